# Optimizing a Trainium2 kernel written in Bass

```python
import math
import jax, jax.numpy as jnp
from jax import lax
import numpy as np


D_MODEL = 1024
BATCH = 8
SEQ = 2048
DEPTH = 4

N_A_LAYERS = DEPTH // 2
N_B_LAYERS = DEPTH - N_A_LAYERS
POOL_WINDOWS = (2, 4, 8, 16)
N_POOL_GROUPS = len(POOL_WINDOWS)
POOL_GROUP = D_MODEL // N_POOL_GROUPS
HEAD_DIM = 64
N_HEADS = D_MODEL // HEAD_DIM
MOBA_BLOCK = 256
MOBA_TOPK = 3
Q_BLOCK = 128
NUM_BUCKETS = 32
MAX_DISTANCE = 128
D_FF = -(-8 * D_MODEL // (3 * 256)) * 256
EPS = 1e-6

kernel_name = 'yoco_pool_moba_hybrid'


def rmsnorm(x, g):
    xf = x.astype(jnp.float32)
    y = xf * lax.rsqrt(jnp.mean(xf * xf, axis=-1, keepdims=True) + EPS)
    return (y * g.astype(jnp.float32)).astype(x.dtype)


def pool_mixer(h, w, scale):
    B, T, D = h.shape
    hf = h.astype(jnp.float32)
    csum = jnp.concatenate([jnp.zeros((B, 1, D), jnp.float32), jnp.cumsum(hf, axis=1)], axis=1)
    t = jnp.arange(T)
    outs = []
    for g, win in enumerate(POOL_WINDOWS):
        sl = slice(g * POOL_GROUP, (g + 1) * POOL_GROUP)
        start = jnp.maximum(t + 1 - win, 0)
        cnt = (t + 1 - start).astype(jnp.float32)
        window_sum = csum[:, 1:, sl] - jnp.take(csum[:, :, sl], start, axis=1)
        outs.append(window_sum / cnt[None, :, None] - hf[:, :, sl])
    d = jnp.stack(outs, axis=2).astype(h.dtype)
    y = jnp.einsum('btgc,gce->btge', d, w).reshape(B, T, D)
    return y * scale


def swiglu(h, w_gate_up, w_down):
    gu = h @ w_gate_up
    gate, up = jnp.split(gu, 2, axis=-1)
    return (jax.nn.silu(gate) * up) @ w_down


def rel_bucket(n):
    max_exact = NUM_BUCKETS // 2
    nf = jnp.maximum(n, max_exact).astype(jnp.float32)
    large = max_exact + (jnp.log(nf / max_exact) / math.log(MAX_DISTANCE / max_exact)
                         * (NUM_BUCKETS - max_exact)).astype(jnp.int32)
    large = jnp.minimum(large, NUM_BUCKETS - 1)
    return jnp.where(n < max_exact, n, large)


def shared_kv(x, g, w_kv):
    B, T, D = x.shape
    h = rmsnorm(x, g)
    kv = h @ w_kv
    k, v = jnp.split(kv, 2, axis=-1)
    n_blocks = -(-T // MOBA_BLOCK)
    pad = n_blocks * MOBA_BLOCK - T

    def to_blocks(a):
        a = a.reshape(B, T, N_HEADS, HEAD_DIM).transpose(0, 2, 1, 3)
        a = jnp.pad(a, ((0, 0), (0, 0), (0, pad), (0, 0)))
        return a.reshape(B, N_HEADS, n_blocks, MOBA_BLOCK, HEAD_DIM)

    k_blocks = to_blocks(k)
    v_blocks = to_blocks(v)
    k_mean = jnp.mean(k_blocks.astype(jnp.float32), axis=3).astype(k_blocks.dtype)
    return k_blocks, v_blocks, k_mean


def moba_attention(q, k_blocks, v_blocks, k_mean, rel_bias):
    B, T = q.shape[0], q.shape[1]
    n_q = T // Q_BLOCK
    n_blocks = k_blocks.shape[2]
    topk = min(MOBA_TOPK, n_blocks)
    scale = HEAD_DIM ** -0.5
    h_idx = jnp.arange(N_HEADS)[:, None, None]
    bias_t = rel_bias.T.astype(jnp.float32)
    q_blocks = q.transpose(0, 2, 1, 3).reshape(B, N_HEADS, n_q, Q_BLOCK, HEAD_DIM)
    q_blocks = q_blocks.transpose(0, 2, 1, 3, 4).reshape(B * n_q, N_HEADS, Q_BLOCK, HEAD_DIM)
    b_idx = jnp.repeat(jnp.arange(B), n_q)
    qb_idx = jnp.tile(jnp.arange(n_q), B)
    j_blk = jnp.arange(MOBA_BLOCK)

    def step(args):
        qblk, b, qb = args
        kb = k_blocks[b]
        vb = v_blocks[b]
        km = k_mean[b]
        qs = qb * Q_BLOCK
        own = qs // MOBA_BLOCK
        qpos = qs + jnp.arange(Q_BLOCK)
        gate = jnp.einsum('hqd,hnd->hqn', qblk, km).astype(jnp.float32)
        gate = jnp.where(jnp.arange(n_blocks)[None, None, :] < own, gate, -jnp.inf)
        _, sel = lax.top_k(gate, topk)
        sel_valid = jnp.arange(topk) < jnp.minimum(own, topk)
        k_sel = kb[h_idx, sel]
        v_sel = vb[h_idx, sel]
        s_sel = jnp.einsum('hqd,hqkjd->hqkj', qblk, k_sel).astype(jnp.float32) * scale
        kpos_sel = sel[..., None] * MOBA_BLOCK + j_blk
        k_own = lax.dynamic_index_in_dim(kb, own, axis=1, keepdims=False)
        v_own = lax.dynamic_index_in_dim(vb, own, axis=1, keepdims=False)
        s_own = jnp.einsum('hqd,hjd->hqj', qblk, k_own).astype(jnp.float32) * scale
        kpos_own = own * MOBA_BLOCK + j_blk
        L_sel = topk * MOBA_BLOCK
        s = jnp.concatenate([s_sel.reshape(N_HEADS, Q_BLOCK, L_sel), s_own], axis=-1)
        kpos = jnp.concatenate([kpos_sel.reshape(N_HEADS, Q_BLOCK, L_sel),
                                jnp.broadcast_to(kpos_own[None, None, :], (N_HEADS, Q_BLOCK, MOBA_BLOCK))], axis=-1)
        valid_sel = jnp.broadcast_to(sel_valid[None, None, :, None], (N_HEADS, Q_BLOCK, topk, MOBA_BLOCK)).reshape(N_HEADS, Q_BLOCK, L_sel)
        valid_own = jnp.broadcast_to(kpos_own[None, None, :] <= qpos[None, :, None], (N_HEADS, Q_BLOCK, MOBA_BLOCK))
        valid = jnp.concatenate([valid_sel, valid_own], axis=-1)
        bucket = rel_bucket(jnp.maximum(qpos[None, :, None] - kpos, 0))
        bias = bias_t[h_idx, bucket]
        s = jnp.where(valid, s + bias, -jnp.inf)
        p = jax.nn.softmax(s, axis=-1).astype(vb.dtype)
        p_sel = p[..., :L_sel].reshape(N_HEADS, Q_BLOCK, topk, MOBA_BLOCK)
        p_own = p[..., L_sel:]
        return (jnp.einsum('hqkj,hqkjd->hqd', p_sel, v_sel)
                + jnp.einsum('hqj,hjd->hqd', p_own, v_own))

    o = lax.map(step, (q_blocks, b_idx, qb_idx))
    o = o.reshape(B, n_q, N_HEADS, Q_BLOCK, HEAD_DIM).transpose(0, 1, 3, 2, 4)
    return o.reshape(B, T, D_MODEL)


def setup_inputs(seed: int = 0) -> dict:
    key = jax.random.key(seed)
    ks = jax.random.split(key, 14)
    f32 = jnp.float32
    nrm = lambda k, shape: jax.random.normal(k, shape, f32)
    x = nrm(ks[0], (BATCH, SEQ, D_MODEL))
    norm_mixer = 1.0 + 0.05 * nrm(ks[1], (DEPTH, D_MODEL))
    norm_ffn = 1.0 + 0.05 * nrm(ks[2], (DEPTH, D_MODEL))
    pool_w = nrm(ks[3], (N_A_LAYERS, N_POOL_GROUPS, POOL_GROUP, POOL_GROUP)) * POOL_GROUP ** -0.5
    pool_scale = 1.0 + 0.05 * nrm(ks[4], (N_A_LAYERS, D_MODEL))
    kv_norm = 1.0 + 0.05 * nrm(ks[5], (D_MODEL,))
    w_kv = nrm(ks[6], (D_MODEL, 2 * D_MODEL)) * D_MODEL ** -0.5
    w_q = nrm(ks[7], (N_B_LAYERS, D_MODEL, D_MODEL)) * D_MODEL ** -0.5
    w_o = nrm(ks[8], (N_B_LAYERS, D_MODEL, D_MODEL)) * D_MODEL ** -0.5
    rel_bias = 0.5 * nrm(ks[9], (NUM_BUCKETS, N_HEADS))
    w_gate_up = nrm(ks[10], (DEPTH, D_MODEL, 2 * D_FF)) * D_MODEL ** -0.5
    w_down = nrm(ks[11], (DEPTH, D_FF, D_MODEL)) * D_FF ** -0.5
    final_norm = 1.0 + 0.05 * nrm(ks[12], (D_MODEL,))
    return {'x': x, 'norm_mixer': norm_mixer, 'norm_ffn': norm_ffn, 'pool_w': pool_w,
            'pool_scale': pool_scale, 'kv_norm': kv_norm, 'w_kv': w_kv, 'w_q': w_q, 'w_o': w_o,
            'rel_bias': rel_bias, 'w_gate_up': w_gate_up, 'w_down': w_down, 'final_norm': final_norm}


def reference(x, norm_mixer, norm_ffn, pool_w, pool_scale, kv_norm, w_kv, w_q, w_o,
              rel_bias, w_gate_up, w_down, final_norm):
    B, T, D = x.shape
    k_blocks = v_blocks = k_mean = None
    for layer in range(DEPTH):
        if layer < N_A_LAYERS:
            x = x + pool_mixer(rmsnorm(x, norm_mixer[layer]), pool_w[layer], pool_scale[layer])
        else:
            j = layer - N_A_LAYERS
            if j == 0:
                k_blocks, v_blocks, k_mean = shared_kv(x, kv_norm, w_kv)
            h = rmsnorm(x, norm_mixer[layer])
            q = (h @ w_q[j]).reshape(B, T, N_HEADS, HEAD_DIM)
            o = moba_attention(q, k_blocks, v_blocks, k_mean, rel_bias)
            x = x + o @ w_o[j]
        x = x + swiglu(rmsnorm(x, norm_ffn[layer]), w_gate_up[layer], w_down[layer])
    return rmsnorm(x, final_norm)
```

```python
import numpy as np
import concourse.bass as bass
import concourse.mybir as mybir
from concourse.bass_utils import run_bass_kernel_spmd

F32 = mybir.dt.float32
BF16 = mybir.dt.bfloat16
AF = mybir.ActivationFunctionType
ALU = mybir.AluOpType
AX = mybir.AxisListType

D = 1024
T = 2048
NB = 8
DFF = 2816
NCH = 8
EPS = 1e-6
N_A = 2
DEPTH = 4
HD = 64
NH = 16

COMPUTE = ("pe", "act", "dve", "pool")
DMAQ_SLOTS = 8


class Op:
    __slots__ = ("eng", "fn", "dma", "deps", "signal", "sigval", "dslot", "dval", "idx", "qi")

    def __init__(self, eng, fn, dma, idx):
        self.eng = eng
        self.fn = fn
        self.dma = dma
        self.deps = []
        self.signal = False
        self.sigval = 0
        self.dslot = 0
        self.dval = 0
        self.idx = idx
        self.qi = 0


class Prog:
    def __init__(self, nc, strict=True):
        self.nc = nc
        self.ops = []
        self.last_w = {}
        self.readers = {}
        self.strict = strict
        self.qcount = {}

    def add(self, eng, fn, reads=(), writes=(), dma=False):
        op = Op(eng, fn, dma, len(self.ops))
        deps = {}

        def need(o):
            if o is op:
                return
            key = o if o.dma else o.eng
            cur = deps.get(key)
            if cur is None or cur.idx < o.idx:
                deps[key] = o

        for t in reads:
            w = self.last_w.get(t)
            if w is not None:
                need(w)
        for t in writes:
            w = self.last_w.get(t)
            if w is not None:
                need(w)
            for r in self.readers.get(t, {}).values():
                need(r)
        for t in reads:
            self.readers.setdefault(t, {})[op if dma else eng] = op
        for t in writes:
            self.last_w[t] = op
            self.readers[t] = {}
        if dma:
            qi = self.qcount.get(eng, 0)
            self.qcount[eng] = qi + 1
            op.qi = qi
            op.dslot = qi % DMAQ_SLOTS
            op.dval = 16 * (qi // DMAQ_SLOTS + 1)
        op.deps = list(deps.values())
        self.ops.append(op)
        return op

    def emit(self):
        nc = self.nc
        ops = self.ops
        prev_slot = {}
        for op in ops:
            if op.dma:
                k = (op.eng, op.dslot)
                p = prev_slot.get(k)
                if p is not None:
                    op.deps.append(p)
                prev_slot[k] = op
        for op in ops:
            for d in op.deps:
                if not d.dma and (d.eng != op.eng or op.dma or (self.strict and op.eng != "pe")):
                    d.signal = True
        cnt = {}
        for op in ops:
            if not op.dma and op.signal:
                cnt[op.eng] = cnt.get(op.eng, 0) + 1
                op.sigval = cnt[op.eng]
        by_eng = {}
        for op in ops:
            by_eng.setdefault(op.eng, []).append(op)

        import contextlib
        with contextlib.ExitStack() as st:
            esem = {e: st.enter_context(nc.semaphore("s_" + e)) for e in COMPUTE}
            dsem = {}
            for q in self.qcount:
                for s in range(DMAQ_SLOTS):
                    dsem[(q, s)] = st.enter_context(nc.semaphore("d_%s_%d" % (q, s)))
            block = st.enter_context(nc.Block())

            def run(ename, engobj):
                waited = {}
                for op in by_eng.get(ename, []):
                    for d in op.deps:
                        if d.dma:
                            sem, val = dsem[(d.eng, d.dslot)], d.dval
                        else:
                            if d.eng == ename and not (op.dma or (self.strict and ename != "pe")):
                                continue
                            sem, val = esem[d.eng], d.sigval
                        key = id(sem)
                        if waited.get(key, 0) >= val:
                            continue
                        waited[key] = val
                        engobj.wait_ge(sem, val)
                    ins = op.fn(engobj)
                    if op.dma:
                        ins.then_inc(dsem[(ename, op.dslot)], 16)
                    elif op.signal:
                        ins.then_inc(esem[ename], 1)
                last = {}
                for op in by_eng.get(ename, []):
                    if op.dma:
                        last[op.dslot] = op.dval
                for s, v in last.items():
                    engobj.wait_ge(dsem[(ename, s)], v)

            @block.tensor
            def _(e):
                run("pe", e)

            @block.scalar
            def _(e):
                run("act", e)

            @block.vector
            def _(e):
                run("dve", e)

            @block.gpsimd
            def _(e):
                run("pool", e)

            @block.sync
            def _(e):
                run("sp", e)


class Alloc:
    def __init__(self, nc, base=16512, limit=229344):
        self.nc = nc
        self.off = base
        self.limit = limit
        self.n = 0

    def t(self, name, shape, dtype):
        sz = 1
        for s in shape[1:]:
            sz *= s
        sz *= 2 if dtype == BF16 else 4
        off = (self.off + 63) // 64 * 64
        self.n += 1
        h = self.nc.alloc_sbuf_tensor_at("%s_%d" % (name, self.n), list(shape), dtype, offset=off)
        self.off = off + sz
        assert self.off <= self.limit, (name, self.off, self.limit)
        return h

    def fork(self):
        a = Alloc(self.nc, self.off, self.limit)
        a.n = self.n + 1000
        return a


NV = 88
DEBUG = False
ATT_STAGE = 9
ATT_HP = 8
ATT_OWN = 8
HACK = 0
USED = {}
LAST = {}


def NM(l):
    return l * 8


def NF(l):
    return 32 + l * 8


def PSC(l):
    return 64 + l * 8


KVO = 80


def tl(j, n=512):
    return slice(j * n, (j + 1) * n)


class Builder:
    def __init__(self, nc, layers, final=True):
        self.nc = nc
        self.P = Prog(nc)
        self.layers = layers
        self.final = final

    def pe(self, fn, r, w):
        return self.P.add("pe", fn, list(r) + ["ARENA"], w)

    def act(self, fn, r, w):
        return self.P.add("act", fn, list(r) + ["ARENA"], w)

    def dve(self, fn, r, w):
        return self.P.add("dve", fn, list(r) + ["ARENA"], w)

    def pool(self, fn, r, w):
        return self.P.add("pool", fn, list(r) + ["ARENA"], w)

    def dma(self, q, out, in_, r, w):
        return self.P.add(q, lambda e: e.dma_start(out=out, in_=in_), list(r) + ["ARENA"], w, dma=True)

    def barrier(self):
        self.P.add("pool", lambda e: e.memset(self.junk[:], 0.0), [], ["ARENA"])

    def dbg(self, name, ap, shape, reads, dt=F32):
        if not DEBUG:
            return
        d = self.nc.dram_tensor("dbg_" + name, list(shape), dt, kind="ExternalOutput").ap()
        self.dbgs["dbg_" + name] = d
        self.dma("sp", d, ap, reads, [("dbg", name)])

    def mm(self, out, lhsT, rhs, start, stop, r, w):
        return self.pe(lambda e: e.matmul(out, lhsT, rhs, start=start, stop=stop), r, w)

    def wload(self, src, stg, stok, wb, btok):
        self.dma("sp", stg, src, [], [stok])
        self.act(lambda e: e.activation(out=wb, in_=stg, func=AF.Copy), [stok], [btok])

    def build(self):
        nc = self.nc
        specs = {"x": [T, D], "vecs": [128, NV], "ident_in": [128, 128], "fw": [128, 64], "final_norm": [1, D],
                 "pool_w": [2, 4, 256, 256], "w_kv": [D, 2 * D], "w_q": [2, D, D], "w_o": [2, D, D],
                 "rel_bias": [32, 16], "gmat": [128, 16, 256], "w_gate_up": [DEPTH, D, 2 * DFF],
                 "w_down": [DEPTH, DFF, D], "kind": [128, T]}

        class Lazy(dict):
            def __missing__(d, name):
                d[name] = nc.dram_tensor(name, list(specs[name]), BF16 if name == "kind" else F32,
                                         kind="ExternalInput").ap()
                return d[name]

        dr = Lazy()
        dr["out"] = nc.dram_tensor("out", [T, D], F32, kind="ExternalOutput").ap()
        dr["kt_d"] = nc.dram_tensor("kt_d", [8, 128, T], BF16).ap()
        dr["v_d"] = nc.dram_tensor("v_d", [8, 128, 16 * 132], BF16).ap()
        self.dr = dr
        self.dbgs = {}

        A = Alloc(nc)
        self.XT = A.t("XT", [128, NCH, T], F32)
        self.ident = A.t("ident", [128, 128], F32)
        self.identb = A.t("identb", [128, 128], BF16)
        self.vecs = A.t("vecs", [128, NV], F32)
        self.epsc = A.t("epsc", [128, 1], F32)
        self.onesb = A.t("onesb", [128, 128], BF16)
        self.RS = A.t("RS", [128, T], F32)
        self.KMBD = A.t("KMBD", [128, 2, 8, 16], BF16)
        self.nb31 = A.t("nb31", [128, 16], F32)
        self.FW = A.t("FW", [128, 64], F32)
        self.junk = A.t("junk", [128, 1], F32)
        self.sqb = A.t("sqb", [128, NCH, 512], BF16)
        self.ps = [nc.alloc_psum_tensor("ps%d" % i, [128, 512], F32) for i in range(8)]

        self.dma("sp", self.ident[:], dr["ident_in"], [], ["ident"])
        self.dma("sp", self.vecs[:], dr["vecs"], [], ["vecs"])
        self.dma("sp", self.FW[:], dr["fw"], [], ["FW"])
        self.dma("sp", self.nb31[:], dr["rel_bias"][31:32, :].partition_broadcast(128).rearrange("p o d -> p (o d)"),
                 [], ["nb31"])
        self.dve(lambda e: e.tensor_scalar(out=self.nb31[:], in0=self.nb31[:], scalar1=-1.0, scalar2=None,
                                           op0=ALU.mult), ["nb31"], ["nb31"])
        self.pool(lambda e: e.memset(self.epsc[:], EPS), [], ["epsc"])
        self.pool(lambda e: e.memset(self.onesb[:], 1.0 / D), [], ["onesb"])
        self.act(lambda e: e.activation(out=self.identb[:], in_=self.ident[:], func=AF.Copy), ["ident"], ["identb"])

        self.load_x(A.fork())
        for ll in self.layers:
            l, part = (ll, "mf") if isinstance(ll, int) else ll
            if "m" in part:
                if l < N_A:
                    self.rs_tiles(A.fork())
                    self.pool_mixer(l, A.fork())
                else:
                    self.rs_tiles(A.fork())
                    if l == N_A:
                        self.kv_phase(A.fork())
                    self.attention(l, A.fork())
            if "k" in part:
                self.rs_tiles()
                self.kv_phase(A.fork())
            if "a" in part:
                self.rs_tiles()
                self.attention(l, A.fork())
            if "f" in part:
                self.rs_tiles(A.fork())
                self.ffn(l, A.fork())
        self.final_out(A.fork())
        self.P.emit()

    def load_x(self, A):
        self.barrier()
        xin = [A.t("xin", [128, D], F32) for _ in range(3)]
        xv = self.dr["x"].rearrange("(n p) d -> n p d", p=128)
        for n in range(T // 128):
            b = xin[n % 3]
            tk = ("xin", n % 3)
            self.dma("sp", b[:], xv[n], [], [tk])
            for half in range(2):
                pi = (n * 2 + half) % 4
                pt = self.ps[pi]
                ptk = ("ps", pi)
                for cc in range(4):
                    c = half * 4 + cc
                    self.pe(lambda e, pt=pt, b=b, c=c, cc=cc: e.transpose(
                        pt[:, cc * 128:(cc + 1) * 128], b[:, c * 128:(c + 1) * 128], self.ident[:]),
                        [tk, "ident"], [ptk])
                outap = self.XT[:, half * 4:(half + 1) * 4, n * 128:(n + 1) * 128]
                inap = pt[:].rearrange("p (c t) -> p c t", c=4)
                wt = [("XT", half * 4 + cc, n // 4) for cc in range(4)]
                if (n + half) % 2 == 0:
                    self.dve(lambda e, o=outap, i=inap: e.tensor_copy(out=o, in_=i), [ptk], wt)
                else:
                    self.act(lambda e, o=outap, i=inap: e.activation(out=o, in_=i, func=AF.Copy), [ptk], wt)

    def final_out(self, A):
        self.barrier()
        gfin = A.t("gfin", [128, D], F32)
        self.dma("sp", gfin[:], self.dr["final_norm"].partition_broadcast(128).rearrange("p o d -> p (o d)"),
                 [], ["gfin"])
        ob = [A.t("ob", [128, D], F32) for _ in range(2)]
        sq = A.t("sqjunk", [128, 512], F32)
        ssum = [A.t("ssum", [128, 2], F32) for _ in range(2)]
        ov = self.dr["out"].rearrange("(n p) d -> n p d", p=128)
        for n in range(T // 128):
            pts = [self.ps[(n % 2) * 2 + i] for i in range(2)]
            ptk = [("ps", (n % 2) * 2 + i) for i in range(2)]
            for c in range(NCH):
                half = c // 4
                self.pe(lambda e, pt=pts[half], c=c, n=n: e.transpose(
                    pt[:, (c % 4) * 128:(c % 4 + 1) * 128], self.XT[:, c, n * 128:(n + 1) * 128], self.ident[:]),
                    [("XT", c, n // 4), "ident"], [ptk[half]])
            s = ssum[n % 2]
            stk = ("ssum", n % 2)
            o = ob[n % 2]
            otk = ("ob", n % 2)
            for half in range(2):
                self.act(lambda e, pt=pts[half], s=s, half=half: e.activation(
                    out=sq[:], in_=pt[:], func=AF.Square, accum_out=s[:, half:half + 1]),
                    [ptk[half]], [stk, "sqjunk"])
            self.dve(lambda e, s=s: e.tensor_tensor(out=s[:, 0:1], in0=s[:, 0:1], in1=s[:, 1:2], op=ALU.add),
                     [stk], [stk])
            self.act(lambda e, s=s: e.activation(out=s[:, 0:1], in_=s[:, 0:1], func=AF.Sqrt,
                                                 bias=self.epsc[:], scale=1.0 / D), [stk, "epsc"], [stk])
            self.dve(lambda e, s=s: e.reciprocal(out=s[:, 0:1], in_=s[:, 0:1]), [stk], [stk])
            for half in range(2):
                self.dve(lambda e, pt=pts[half], s=s, o=o, half=half: e.scalar_tensor_tensor(
                    out=o[:, half * 512:(half + 1) * 512], in0=pt[:], scalar=s[:, 0:1],
                    in1=gfin[:, half * 512:(half + 1) * 512], op0=ALU.mult, op1=ALU.mult),
                    [ptk[half], stk, "gfin"], [otk])
            self.dma("sp", ov[n], o[:], [otk], [("out", n)])

    def rs_tiles(self, A=None):
        for j in range(4):
            b = self.sqb
            for c in range(NCH):
                xa = self.XT[:, c, tl(j)]
                if c in (0, 3, 6):
                    self.act(lambda e, o=b[:, c, :], xa=xa: e.activation(out=o, in_=xa, func=AF.Square),
                             [("XT", c, j)], [("sqb", c)])
                else:
                    fn = lambda e, o=b[:, c, :], xa=xa: e.tensor_tensor(out=o, in0=xa, in1=xa, op=ALU.mult)
                    (self.pool if c in (1, 5) else self.dve)(fn, [("XT", c, j)], [("sqb", c)])
            pb = 6 + (j % 2)
            for c in range(NCH):
                self.mm(self.ps[pb][:], self.onesb[:], b[:, c, :], c == 0, c == NCH - 1,
                        [("sqb", c), "onesb"], [("ps", pb)])
            self.act(lambda e, pb=pb, j=j: e.activation(out=self.RS[:, tl(j)], in_=self.ps[pb][:], func=AF.Ln,
                                                        bias=self.epsc[:], scale=1.0),
                     [("ps", pb), "epsc"], [("RS", j)])
            self.act(lambda e, j=j: e.activation(out=self.RS[:, tl(j)], in_=self.RS[:, tl(j)], func=AF.Exp,
                                                 scale=-0.5), [("RS", j)], [("RS", j)])

    def make_hT(self, HT, name, goff, j, jj):
        for c in range(NCH):
            self.dve(lambda e, c=c: e.scalar_tensor_tensor(
                out=HT[:, c, tl(jj)], in0=self.XT[:, c, tl(j)], scalar=self.vecs[:, goff + c:goff + c + 1],
                in1=self.RS[:, tl(j)], op0=ALU.mult, op1=ALU.mult),
                [("XT", c, j), ("RS", j), "vecs"], [(name, c, jj)])

    def ffn(self, l, A):
        self.barrier()
        dr = self.dr
        HT = A.t("HTF", [128, NCH, 1024], BF16)
        ACTT = A.t("ACTT", [128, 22, 1024], BF16)
        stg = [A.t("gstg", [128, NCH, 256], F32) for _ in range(2)]
        wb = [A.t("gwb", [128, NCH, 256], BF16) for _ in range(4)]
        dstg = A.t("dstg", [128, 22, 128], F32)
        dwb = [A.t("dwb", [128, 22, 128], BF16) for _ in range(2)]
        sig = [A.t("sig", [128, 512], F32) for _ in range(2)]
        wgu = dr["w_gate_up"][l].rearrange("(c p) f -> p c f", p=128)
        wd = dr["w_down"][l].rearrange("(k p) d -> p k d", p=128)
        units = []
        for g in range(11):
            units += [(g, 0), (g, 1)]

        def load_unit(i):
            g, gu = units[i]
            col = gu * DFF + g * 256
            self.wload(wgu[:, :, col:col + 256], stg[i % 2][:], ("gstg", i % 2), wb[i % 4][:], ("gwb", i % 4))

        def load_d(oc):
            self.wload(wd[:, :, oc * 128:(oc + 1) * 128], dstg[:], "dstg", dwb[oc % 2][:], ("dwb", oc % 2))

        load_unit(0)
        load_unit(1)
        for jj in range(2):
            self.make_hT(HT, "HTF", NF(l), jj, jj)
        for s in range(2):
            cnt = 0
            for g in range(11):
                if g == 8:
                    load_d(0)
                for k in (2 * g + 2, 2 * g + 3):
                    if k < 22:
                        load_unit(k)
                ig, iu = 2 * g, 2 * g + 1
                for fc in range(2):
                    f = 2 * g + fc
                    for tt in range(2):
                        r = cnt % 2
                        cnt += 1
                        pg, pu = r, 2 + r
                        for (pbk, iw) in ((pg, ig), (pu, iu)):
                            for c in range(NCH):
                                self.mm(self.ps[pbk][:], wb[iw % 4][:, c, fc * 128:(fc + 1) * 128], HT[:, c, tl(tt)],
                                        c == 0, c == NCH - 1, [("gwb", iw % 4), ("HTF", c, tt)], [("ps", pbk)])
                        self.act(lambda e, r=r, pg=pg: e.activation(out=sig[r][:], in_=self.ps[pg][:], func=AF.Silu),
                                 [("ps", pg)], [("sig", r)])
                        self.dve(lambda e, r=r, pu=pu, f=f, tt=tt: e.tensor_tensor(
                            out=ACTT[:, f, tl(tt)], in0=sig[r][:], in1=self.ps[pu][:], op=ALU.mult),
                            [("sig", r), ("ps", pu)], [("ACTT", f, tt)])

            if s == 0:
                load_unit(0)
                load_unit(1)
                for jj in range(2):
                    self.make_hT(HT, "HTF", NF(l), 2 + jj, jj)
            for oc in range(8):
                if oc + 1 < 8:
                    load_d(oc + 1)
                for tt in range(2):
                    pb = 4 + (oc * 2 + tt) % 2
                    j = 2 * s + tt
                    for k in range(22):
                        self.mm(self.ps[pb][:], dwb[oc % 2][:, k, :], ACTT[:, k, tl(tt)], k == 0, k == 21,
                                [("dwb", oc % 2), ("ACTT", k, tt)], [("ps", pb)])
                    self.dve(lambda e, pb=pb, oc=oc, j=j: e.tensor_tensor(
                        out=self.XT[:, oc, tl(j)], in0=self.ps[pb][:], in1=self.XT[:, oc, tl(j)], op=ALU.add),
                        [("ps", pb), ("XT", oc, j)], [("XT", oc, j)])

    def pool_mixer(self, l, A):
        self.barrier()
        dr = self.dr
        HCs = [A.t("HC", [128, 16 + T], F32) for _ in range(2)]
        Ss = [[A.t("SW", [128, 16 + T], F32) for _ in range(2)] for _ in range(2)]
        DT = [A.t("DT", [128, 2, T], BF16) for _ in range(2)]
        pstg = A.t("pstg", [128, 8, 256], F32)
        PW = A.t("PW", [128, 8, 256], BF16)
        tmp16s = [A.t("tmp16", [128, 16], F32) for _ in range(2)]
        self.wload(dr["pool_w"][l].rearrange("g (ci p) e -> p (g ci) e", p=128), pstg[:], "pstg", PW[:], "PW")
        for q in range(2):
            self.pool(lambda e, q=q: e.memset(HCs[q][:, 0:16], 0.0), [], [("HC", q)])
            for i in range(2):
                self.pool(lambda e, q=q, i=i: e.memset(Ss[q][i][:, 0:16], 0.0), [], [("SW", q, i)])
        allrs = [("RS", j) for j in range(4)]

        def emit_HC(g, ci):
            c = 2 * g + ci
            HC, hctk = HCs[ci], ("HC", ci)
            self.dve(lambda e, c=c, HC=HC: e.scalar_tensor_tensor(
                out=HC[:, 16:], in0=self.XT[:, c, :], scalar=self.vecs[:, NM(l) + c:NM(l) + c + 1],
                in1=self.RS[:, :], op0=ALU.mult, op1=ALU.mult),
                [("XT", c, j) for j in range(4)] + allrs + ["vecs"], [hctk])

        def emit_adds(g, ci):
            w = 2 << g
            S = Ss[ci]
            src, stok = HCs[ci], ("HC", ci)
            sh = 1
            k = 0
            while sh < w:
                dst, dtok = S[k % 2], ("SW", ci, k % 2)
                fn = lambda e, dst=dst, src=src, sh=sh: e.tensor_tensor(
                    out=dst[:, 16:], in0=src[:, 16:], in1=src[:, 16 - sh:16 - sh + T], op=ALU.add)
                (self.dve if (g == 3 and sh * 2 >= w) else self.pool)(fn, [stok], [dtok])
                src, stok = dst, dtok
                sh *= 2
                k += 1
            return src, stok

        def emit_D(g, ci, src, stok):
            w = 2 << g
            dt = DT[g % 2]
            HC, hctk = HCs[ci], ("HC", ci)
            tmp16, t16tk = tmp16s[ci], ("tmp16", ci)
            self.dve(lambda e, src=src, ci=ci, dt=dt, w=w, HC=HC: e.scalar_tensor_tensor(
                out=dt[:, ci, :], in0=src[:, 16:], scalar=1.0 / w, in1=HC[:, 16:], op0=ALU.mult,
                op1=ALU.subtract), [stok, hctk], [("DT", g % 2, ci)])
            self.dve(lambda e, src=src, g=g, tmp16=tmp16: e.tensor_tensor(
                out=tmp16[:], in0=src[:, 16:32], in1=self.FW[:, g * 16:(g + 1) * 16], op=ALU.mult),
                [stok, "FW"], [t16tk])
            self.dve(lambda e, ci=ci, dt=dt, tmp16=tmp16, HC=HC: e.tensor_tensor(
                out=dt[:, ci, 0:16], in0=tmp16[:], in1=HC[:, 16:32], op=ALU.subtract),
                [t16tk, hctk], [("DT", g % 2, ci)])

        def emit_mm(g):
            dt = DT[g % 2]
            for eo in range(2):
                co = 2 * g + eo
                for j in range(4):
                    pb = (eo * 4 + j) % 4
                    for ci in range(2):
                        self.mm(self.ps[pb][:], PW[:, 2 * g + ci, eo * 128:(eo + 1) * 128], dt[:, ci, tl(j)],
                                ci == 0, ci == 1, ["PW", ("DT", g % 2, ci)], [("ps", pb)])
                    self.dve(lambda e, pb=pb, co=co, j=j: e.scalar_tensor_tensor(
                        out=self.XT[:, co, tl(j)], in0=self.ps[pb][:],
                        scalar=self.vecs[:, PSC(l) + co:PSC(l) + co + 1], in1=self.XT[:, co, tl(j)],
                        op0=ALU.mult, op1=ALU.add), [("ps", pb), ("XT", co, j), "vecs"], [("XT", co, j)])

        emit_HC(0, 0)
        emit_HC(0, 1)
        for g in range(4):
            r0 = emit_adds(g, 0)
            r1 = emit_adds(g, 1)
            emit_D(g, 0, *r0)
            if g < 3:
                emit_HC(g + 1, 0)
            emit_D(g, 1, *r1)
            if g < 3:
                emit_HC(g + 1, 1)
            emit_mm(g)

    def kv_phase(self, A):
        self.barrier()
        dr = self.dr
        HT = A.t("HTK", [128, NCH, T], BF16)
        kstg = [A.t("kstg", [128, NCH, 128], F32) for _ in range(2)]
        kwb = [A.t("kwb", [128, NCH, 128], BF16) for _ in range(3)]
        KTB = [A.t("KTB", [128, T], BF16) for _ in range(2)]
        VTS = [A.t("VTS", [128, T], BF16) for _ in range(2)]
        VB = [A.t("VB", [128, 16, 132], BF16) for _ in range(2)]
        kmf = A.t("kmf", [128, 8], F32)
        wkv = dr["w_kv"].rearrange("(c p) f -> p c f", p=128)
        units = []
        for hp in range(ATT_HP):
            units += [(hp, 0), (hp, 1)]

        def load(i):
            hp, kv = units[i]
            col = kv * D + hp * 128
            self.wload(wkv[:, :, col:col + 128], kstg[i % 2][:], ("kstg", i % 2), kwb[i % 3][:], ("kwb", i % 3))

        load(0)
        load(1)
        for jx in range(4):
            self.make_hT(HT, "HTK", KVO, jx, jx)
        for i in range(2):
            self.pool(lambda e, i=i: e.memset(VB[i][:], 1.0), [], [("VB", i)])
        self.pool(lambda e: e.memset(self.KMBD[:], 0.0), [], ["KMBD"])
        nb = 0
        for i, (hp, kv) in enumerate(units):
            if i + 2 < len(units):
                load(i + 2)
            w = kwb[i % 3]
            wtk = ("kwb", i % 3)
            if kv == 0:
                ktb = KTB[hp % 2]
                for jx in range(4):
                    pb = nb % 4
                    nb += 1
                    for c in range(NCH):
                        self.mm(self.ps[pb][:], w[:, c, :], HT[:, c, tl(jx)], c == 0, c == NCH - 1,
                                [wtk, ("HTK", c, jx)], [("ps", pb)])
                    self.act(lambda e, pb=pb, jx=jx, ktb=ktb: e.activation(out=ktb[:, tl(jx)], in_=self.ps[pb][:],
                                                                            func=AF.Copy),
                             [("ps", pb)], [("KTB", hp % 2)])
                self.dma("sp", dr["kt_d"][hp], ktb[:], [("KTB", hp % 2)], [("kt_d", hp)])
                self.dve(lambda e, ktb=ktb: e.tensor_reduce(
                    out=kmf[:, 0:8], in_=ktb[:].rearrange("p (n k) -> p n k", k=256), axis=AX.X, op=ALU.add),
                    [("KTB", hp % 2)], ["kmf"])
                self.dve(lambda e, hp=hp: e.tensor_scalar(out=self.KMBD[0:64, 0, hp, 0:8], in0=kmf[0:64, :],
                                                          scalar1=1.0 / 256, scalar2=None, op0=ALU.mult),
                         ["kmf"], ["KMBD"])
                self.dve(lambda e, hp=hp: e.tensor_scalar(out=self.KMBD[64:128, 1, hp, 8:16], in0=kmf[64:128, :],
                                                          scalar1=1.0 / 256, scalar2=None, op0=ALU.mult),
                         ["kmf"], ["KMBD"])
            else:
                vts = VTS[hp % 2]
                vb = VB[hp % 2]
                for jx in range(4):
                    pb = nb % 4
                    nb += 1
                    for c in range(NCH):
                        self.mm(self.ps[pb][:], w[:, c, :], HT[:, c, tl(jx)], c == 0, c == NCH - 1,
                                [wtk, ("HTK", c, jx)], [("ps", pb)])
                    self.dve(lambda e, pb=pb, jx=jx, vts=vts: e.tensor_copy(out=vts[:, tl(jx)], in_=self.ps[pb][:]),
                             [("ps", pb)], [("VTS", hp % 2, jx)])
                for half in range(2):
                    bank = 4 + half
                    psb = self.ps[bank][:].bitcast(BF16)
                    for t8 in range(8):
                        tk = half * 8 + t8
                        self.pe(lambda e, psb=psb, t8=t8, tk=tk, vts=vts: e.transpose(
                            psb[:, t8 * 128:(t8 + 1) * 128], vts[:, tk * 128:(tk + 1) * 128], self.identb[:]),
                            [("VTS", hp % 2, tk // 4), "identb"], [("ps", bank)])
                    oap = vb[:, half * 8:(half + 1) * 8, :].rearrange("p t (h e) -> p t h e", e=66)[:, :, :, 0:64]
                    iap = psb[:, 0:1024].rearrange("p (t h e) -> p t h e", t=8, h=2)
                    if half == 0:
                        self.act(lambda e, oap=oap, iap=iap: e.activation(out=oap, in_=iap, func=AF.Copy),
                                 [("ps", bank)], [("VB", hp % 2)])
                    else:
                        self.dve(lambda e, oap=oap, iap=iap: e.tensor_copy(out=oap, in_=iap),
                                 [("ps", bank)], [("VB", hp % 2)])
                for hf in range(2):
                    self.dma("sp", dr["v_d"][hp][:, hf * 1056:(hf + 1) * 1056],
                             vb[:, hf * 8:(hf + 1) * 8, :].rearrange("p t e -> p (t e)"), [("VB", hp % 2)],
                             [("v_d", hp, hf)])

    def attention(self, l, A):
        self.barrier()
        dr = self.dr
        jl = l - N_A
        PTb = [A.t("PT", [128, 8, 2, 256], BF16) for _ in range(2)]
        PF = A.t("PF", [128, 512], F32)
        PFS = A.t("PFS", [128, 128], F32)
        ON = [A.t("ON", [128, 2, 128], BF16) for _ in range(2)]
        HT = A.t("HTQ", [128, NCH, T], BF16)
        OT = A.t("OT", [128, NCH, T], BF16)
        KTv = [A.t("KTA", [128, T], BF16), A.t("KTB", [128, T], BF16)]
        QTa = [A.t("QT", [128, T], BF16) for _ in range(2)]
        QTb = [[QTa[0][:], QTa[1][:]],
               [self.sqb[:, 0:4, :].rearrange("p c t -> p (c t)"), self.sqb[:, 4:8, :].rearrange("p c t -> p (c t)")]]
        V = [A.t("V", [128, 16, 132], BF16) for _ in range(2)]
        qstg = A.t("qstg", [128, NCH, 64], F32)
        GST = A.t("GST", [128, 2, 256], F32)
        qwb = [A.t("qwb", [128, NCH, 128], BF16) for _ in range(2)]
        EG = A.t("EG", [128, 2, 2, 256], F32)
        RINV = [A.t("RINV", [128, 2], F32) for _ in range(2)]
        GT = A.t("GT", [128, 16, 8], F32)
        GT2 = A.t("GT2", [128, 16, 8], F32)
        MK = A.t("MK", [128, 16, 8], F32)
        MX = A.t("MX", [128, 16], F32)
        SEL = A.t("SEL", [128, 16, 8], F32)
        INV = A.t("INV", [128, 16, 8], F32)
        M128 = A.t("M128", [128, 4, 2, 128], BF16)
        for j in range(4):
            self.make_hT(HT, "HTQ", NM(l), j, j)
        self.pool(lambda e: e.memset(EG[:, :, 1, 0:128], 0.0), [], ["EG"])
        for b in range(2):
            self.pool(lambda e, b=b: e.memset(QTb[b][0][64:128, :], 0.0), [("sqb", c) for c in range(NCH)],
                      [("QT", b, o) for o in range(8)] + [("sqb", c) for c in range(NCH)])
            self.pool(lambda e, b=b: e.memset(QTb[b][1][0:64, :], 0.0), [], [("QT", b, o) for o in range(8)])
        self.pool(lambda e: e.memset(M128[:], 0.0), [], ["M128"])
        self.pool(lambda e: e.memset(INV[:], 0.0), [], ["INV"])
        for o in range(4, 8):
            self.pool(lambda e, o=o: e.memset(INV[:, (o - 4) * 4:(o - 4) * 4 + 4, o:8], 1.0), [], ["INV"])
        self.dma("sp", KTv[0][64:128, :], dr["kind"][64:128, :], [], ["KTA"])
        self.dma("sp", KTv[1][0:64, :], dr["kind"][0:64, :], [], ["KTB"])
        wq = dr["w_q"][jl].rearrange("(c p) f -> p c f", p=128)
        wo = dr["w_o"][jl].rearrange("(c p) f -> p c f", p=128)
        ps7b = self.ps[7][:].bitcast(BF16)
        ps6b = self.ps[6][:].bitcast(BF16)
        cnt = {"sb": 0, "slot": 0, "pf": 0, "unit": 0}
        for hp in range(ATT_HP):
            v = V[hp % 2]
            vtk = ("V", hp % 2)
            self.dma("sp", KTv[0][0:64, :], dr["kt_d"][hp][0:64, :], [("kt_d", hp)], ["KTA"])
            self.dma("sp", KTv[1][64:128, :], dr["kt_d"][hp][64:128, :], [("kt_d", hp)], ["KTB"])
            for hf in range(2):
                self.dma("sp", v[:, hf * 8:(hf + 1) * 8, :].rearrange("p t e -> p (t e)"),
                         dr["v_d"][hp][:, hf * 1056:(hf + 1) * 1056], [("v_d", hp, hf)], [vtk])
            if hp == 0:
                self.dma("sp", GST[:], dr["gmat"][:, 0:2, :], [], ["GST"])
            for hh in range(2):
                h = 2 * hp + hh
                self.act(lambda e, hh=hh, h=h: e.activation(out=EG[:, hh, 0, :], in_=GST[:, hh, :], func=AF.Exp,
                                                            bias=self.nb31[:, h:h + 1], scale=1.0),
                         ["GST", "nb31"], ["EG"])
                self.act(lambda e, hh=hh: e.activation(out=EG[:, hh, 1, 128:256], in_=EG[:, hh, 0, 0:128],
                                                       func=AF.Copy), ["EG"], ["EG"])
            if hp + 1 < 8:
                self.dma("sp", GST[:], dr["gmat"][:, 2 * hp + 2:2 * hp + 4, :], [], ["GST"])

            def prologue(hp):
                QT = QTb[hp % 2]
                qb = hp % 2
                chunks = []

                def c0():
                    for hf in range(2):
                        self.wload(wq[:, :, hp * 128 + hf * 64:hp * 128 + (hf + 1) * 64], qstg[:], "qstg",
                                   qwb[hp % 2][:, :, hf * 64:(hf + 1) * 64], ("qwb", hp % 2))
                chunks.append(c0)

                def cj(j):
                    def f():
                        for c in range(NCH):
                            self.mm(self.ps[7][:], qwb[hp % 2][:, c, :], HT[:, c, tl(j)], c == 0, c == NCH - 1,
                                    [("qwb", hp % 2), ("HTQ", c, j)], [("ps", 7)])
                        for hq in range(2):
                            self.dve(lambda e, hq=hq: e.tensor_scalar(
                                out=QT[hq][64 * hq:64 * hq + 64, tl(j)], in0=self.ps[7][64 * hq:64 * hq + 64, :],
                                scalar1=0.125, scalar2=None, op0=ALU.mult), [("ps", 7)],
                                [("QT", qb, 2 * j), ("QT", qb, 2 * j + 1)])
                    return f
                for j in range(4):
                    chunks.append(cj(j))

                def cg():
                    for o in range(4, 8):
                        for qs in range(2):
                            col = (o - 4) * 32 + qs * 16
                            for hq in range(2):
                                self.mm(self.ps[7][:, col:col + 16],
                                        QT[hq][:, o * 256 + qs * 128:o * 256 + (qs + 1) * 128],
                                        self.KMBD[:, hq, hp, :], hq == 0, hq == 1, [("QT", qb, o), "KMBD"],
                                        [("ps", 7)])
                    self.pool(lambda e: e.memset(GT[:], -1e30), [], ["GT"])
                    for o in range(4, 8):
                        self.dve(lambda e, o=o: e.tensor_copy(
                            out=GT[:, (o - 4) * 4:(o - 4) * 4 + 4, 0:o],
                            in_=self.ps[7][:, (o - 4) * 32:(o - 4) * 32 + 32].rearrange(
                                "p (a n) -> p a n", n=8)[:, :, 0:o]), [("ps", 7)], ["GT"])
                    cur = GT
                    for it in range(3):
                        self.dve(lambda e, cur=cur: e.tensor_reduce(out=MX[:], in_=cur[:], axis=AX.X, op=ALU.max),
                                 ["GT"], ["GT"])
                        if it == 2:
                            break
                        self.dve(lambda e, cur=cur: e.tensor_tensor(
                            out=MK[:], in0=cur[:], in1=MX[:].unsqueeze(2).broadcast_to([128, 16, 8]), op=ALU.is_ge),
                            ["GT"], ["GT"])
                        self.dve(lambda e, cur=cur: e.scalar_tensor_tensor(
                            out=GT2[:], in0=MK[:], scalar=-1e30, in1=cur[:], op0=ALU.mult, op1=ALU.add),
                            ["GT"], ["GT"])
                        cur = GT2
                    self.dve(lambda e: e.tensor_tensor(
                        out=SEL[:], in0=GT[:], in1=MX[:].unsqueeze(2).broadcast_to([128, 16, 8]), op=ALU.is_ge),
                        ["GT"], ["SEL"])
                    self.dve(lambda e: e.tensor_tensor(out=SEL[:], in0=SEL[:], in1=INV[:], op=ALU.max),
                             ["SEL", "INV"], ["SEL"])
                    selv = SEL[:].rearrange("p (o q h) n -> p o q h n", o=4, q=2)
                    for hh in range(2):
                        base = 64 if hh == 0 else 0
                        self.dve(lambda e, hh=hh, base=base: e.tensor_scalar(
                            out=M128[:, :, :, base:base + 8], in0=selv[:, :, :, hh, :], scalar1=-1.0,
                            scalar2=30000.0, op0=ALU.add, op1=ALU.mult), ["SEL"], ["M128"])
                chunks.append(cg)
                return chunks

            if hp == 0:
                for ch in prologue(0):
                    ch()
            next_chunks = prologue(hp + 1) if hp + 1 < 8 else []
            QT = QTb[hp % 2]
            qb = hp % 2
            nsel = 4
            def emit_mask_rows():
                for o in range(4, 4 + nsel):
                    for qs in range(2):
                        cb = (o - 4) * 256 + qs * 128
                        self.pe(lambda e, o=o, qs=qs, cb=cb: e.transpose(ps7b[:, cb:cb + 128], M128[:, o - 4, qs, :],
                                                                         self.identb[:]),
                                ["M128", "identb"], [("ps", 7)])
                n = nsel * 256
                oa = QT[0][64:72, 1024:1024 + n]
                ob = QT[1][0:8, 1024:1024 + n]
                ia = ps7b[64:72, 0:n]
                ib = ps7b[0:8, 0:n]
                wtk = [("QT", qb, o) for o in range(4, 4 + nsel)]
                self.act(lambda e, oa=oa, ia=ia: e.activation(out=oa, in_=ia, func=AF.Copy), [("ps", 7)], wtk)
                self.act(lambda e, ob=ob, ib=ib: e.activation(out=ob, in_=ib, func=AF.Copy), [("ps", 7)], wtk)

            own_order = [0, 7, 1, 6, 2, 5, 3, 4]
            units = []
            for a, b in ((0, 7), (1, 6), (2, 5), (3, 4)):
                units += [(a, 0), (b, 0), (a, 1), (b, 1)]
            onidx = {own: k % 2 for k, own in enumerate(own_order)}

            def emit_block_scores(own, hh, n, pbi):
                q0 = own * 256
                PT = PTb[pbi]
                sb = cnt["sb"] % 3
                cnt["sb"] += 1
                for k2 in range(2):
                    kk = n * 256 + k2 * 128
                    self.mm(self.ps[sb][:, k2 * 256:(k2 + 1) * 256], KTv[hh][:, kk:kk + 128],
                            QT[hh][:, q0:q0 + 256], True, True, ["KTA" if hh == 0 else "KTB", ("QT", qb, own)],
                            [("ps", sb)])
                ptk = ("PT", pbi, n)
                if n == own:
                    self.act(lambda e, sb=sb: e.activation(out=PF[:], in_=self.ps[sb][:], func=AF.Exp),
                             [("ps", sb)], ["PF"])
                    self.dve(lambda e, PT=PT, n=n, hh=hh: e.tensor_tensor(
                        out=PT[:, n, :, :], in0=PF[:].rearrange("p (a b) -> p a b", a=2),
                        in1=EG[:, hh, :, :], op=ALU.mult), ["PF", "EG"], [ptk])
                elif n == own - 1:
                    self.act(lambda e, sb=sb, PT=PT, n=n: e.activation(
                        out=PT[:, n, :, :], in_=self.ps[sb][:].rearrange("p (a b) -> p a b", a=2),
                        func=AF.Exp), [("ps", sb)], [ptk])
                    self.act(lambda e, sb=sb: e.activation(out=PFS[:, 0:128], in_=self.ps[sb][:, 256:384],
                                                          func=AF.Exp), [("ps", sb)], ["PFS"])
                    self.dve(lambda e, PT=PT, n=n, hh=hh: e.tensor_tensor(
                        out=PT[:, n, 1, 0:128], in0=PFS[:, 0:128], in1=EG[:, hh, 0, 128:256], op=ALU.mult),
                        ["PFS", "EG"], [ptk])
                else:
                    self.act(lambda e, sb=sb, PT=PT, n=n: e.activation(
                        out=PT[:, n, :, :], in_=self.ps[sb][:].rearrange("p (a b) -> p a b", a=2),
                        func=AF.Exp), [("ps", sb)], [ptk])

            def emit_pv_block(u, n):
                own, hh, pbi, par = u
                PT = PTb[pbi]
                for qs in range(2):
                    bank = 3 + 2 * par + qs
                    for k2 in range(2):
                        if n == own and k2 == 1 and qs == 0:
                            continue
                        bo = border(own)
                        first = (n == bo[0] and k2 == 0)
                        last = (n == bo[-1] and k2 == (1 if (qs == 1 or n != own) else 0))
                        self.mm(self.ps[bank][:, 0:66], PT[:, n, k2, qs * 128:(qs + 1) * 128],
                                v[:, n * 2 + k2, hh * 66:(hh + 1) * 66], first, last,
                                [("PT", pbi, n), vtk], [("ps", bank)])

            def emit_pv_finish(u, hp=hp):
                own, hh, pbi, par = u
                on = ON[onidx[own]]
                ontk = ("ON", onidx[own])
                rinv = RINV[par]
                rtk = ("RINV", par)
                for qs in range(2):
                    bank = 3 + 2 * par + qs
                    self.dve(lambda e, bank=bank, qs=qs, rinv=rinv: e.reciprocal(
                        out=rinv[:, qs:qs + 1], in_=self.ps[bank][:, 64:65]), [("ps", bank)], [rtk])
                    self.dve(lambda e, bank=bank, qs=qs, rinv=rinv, on=on, hh=hh: e.tensor_scalar(
                        out=on[:, qs, hh * 64:(hh + 1) * 64], in0=self.ps[bank][:, 0:64], scalar1=rinv[:, qs:qs + 1],
                        scalar2=None, op0=ALU.mult), [("ps", bank), rtk], [ontk])
                if hh == 1:
                    q0 = own * 256

                    def tr(on=on, ontk=ontk, q0=q0, hp=hp, own=own):
                        for qs in range(2):
                            self.pe(lambda e, on=on, qs=qs: e.transpose(ps7b[:, qs * 128:(qs + 1) * 128],
                                                                         on[:, qs, :], self.identb[:]),
                                    [ontk, "identb"], [("ps", 7)])
                        self.dve(lambda e, q0=q0, hp=hp: e.tensor_copy(out=OT[:, hp, q0:q0 + 256],
                                                                       in_=ps7b[:, 0:256]),
                                 [("ps", 7)], [("OT", hp, own // 2)])
                    pending.append(tr)

            prev = None
            pending = []

            def border(own):
                return ([own] + ([own - 1] if own >= 1 else []) + list(range(0, own - 1)))
            emit_mask_rows()
            LAG = 2
            for (own, hh) in units:
                par = cnt["unit"] % 2
                cnt["unit"] += 1
                pbi = par
                order = border(own)
                porder = border(prev[0]) if prev is not None else []
                npv = 0
                for i, n in enumerate(order):
                    emit_block_scores(own, hh, n, pbi)
                    if i >= LAG and npv < len(porder):
                        emit_pv_block(prev, porder[npv])
                        npv += 1
                    if own >= 5 and i in (3, 5) and next_chunks:
                        next_chunks.pop(0)()
                if prev is not None:
                    for n in porder[npv:]:
                        emit_pv_block(prev, n)
                    if pending:
                        pending.pop(0)()
                    emit_pv_finish(prev)
                prev = (own, hh, pbi, par)
            if prev is not None:
                for n in border(prev[0]):
                    emit_pv_block(prev, n)
                emit_pv_finish(prev)
            while next_chunks:
                next_chunks.pop(0)()
            while pending:
                pending.pop(0)()
        def load_o(oc):
            for hf in range(2):
                self.wload(wo[:, :, oc * 128 + hf * 64:oc * 128 + (hf + 1) * 64], qstg[:], "qstg",
                           qwb[oc % 2][:, :, hf * 64:(hf + 1) * 64], ("qwb", oc % 2))

        load_o(0)
        for oc in range(8):
            if oc + 1 < 8:
                load_o(oc + 1)
            for j in range(4):
                pb = 6 + j % 2
                for c in range(NCH):
                    self.mm(self.ps[pb][:], qwb[oc % 2][:, c, :], OT[:, c, tl(j)], c == 0, c == NCH - 1,
                            [("qwb", oc % 2), ("OT", c, j)], [("ps", pb)])
                self.dve(lambda e, pb=pb, oc=oc, j=j: e.tensor_tensor(
                    out=self.XT[:, oc, tl(j)], in0=self.ps[pb][:], in1=self.XT[:, oc, tl(j)], op=ALU.add),
                    [("ps", pb), ("XT", oc, j)], [("XT", oc, j)])
        self.barrier()


_CACHE = {}


def _build(layers, final=True):
    key = (tuple(layers), final, DEBUG, ATT_STAGE, ATT_HP, ATT_OWN, HACK)
    if key not in _CACHE:
        nc = bass.Bass("TRN2", target_bir_lowering=False)
        b = Builder(nc, layers, final)
        b.build()
        _CACHE[key] = (nc, set(k for k in b.dr.keys()), list(b.dbgs.keys()))
    return _CACHE[key]


def _rel_bucket(n):
    n = np.asarray(n)
    nf = np.maximum(n, 16).astype(np.float32)
    large = 16 + (np.log(nf / np.float32(16)) / np.float32(np.log(128 / 16)) * np.float32(16)).astype(np.int32)
    large = np.minimum(large, 31)
    return np.where(n < 16, n, large)


def _host_inputs(norm_mixer, norm_ffn, pool_scale, kv_norm, rel_bias):
    f32 = np.float32
    cols = [np.asarray(norm_mixer, f32).reshape(4, 8, 128), np.asarray(norm_ffn, f32).reshape(4, 8, 128),
            np.asarray(pool_scale, f32).reshape(2, 8, 128), np.asarray(kv_norm, f32).reshape(1, 8, 128)]
    vecs = np.concatenate([c.reshape(-1, 128) for c in cols], axis=0).T.copy()
    assert vecs.shape == (128, NV)
    fw = np.zeros((128, 64), f32)
    for g in range(4):
        w = 2 << g
        fw[:, g * 16:(g + 1) * 16] = 1.0 / np.minimum(np.arange(16) + 1, w)
    p = np.arange(128)[:, None]
    c = np.arange(256)[None, :]
    dist = c - p
    bucket = _rel_bucket(np.maximum(dist, 0))
    rb = np.asarray(rel_bias, f32)
    gm = rb[bucket]
    gm = np.where((dist >= 0)[:, :, None], gm, f32(-30000.0))
    gmat = np.ascontiguousarray(gm.transpose(0, 2, 1)).astype(f32)
    return vecs, fw, gmat


def kernel(x, norm_mixer, norm_ffn, pool_w, pool_scale, kv_norm, w_kv, w_q, w_o,
           rel_bias, w_gate_up, w_down, final_norm, _layers=None):
    layers = list(range(DEPTH)) if _layers is None else _layers
    nc, used, dbgs = _build(layers)
    f32 = np.float32
    x = np.ascontiguousarray(x, dtype=f32)
    vecs, fw, gmat = _host_inputs(norm_mixer, norm_ffn, pool_scale, kv_norm, rel_bias)
    import ml_dtypes
    kind = np.zeros((128, T), f32)
    for n in range(8):
        kind[n, n * 256:(n + 1) * 256] = 1.0
        kind[64 + n, n * 256:(n + 1) * 256] = 1.0
    shared = {
        "vecs": vecs, "ident_in": np.eye(128, dtype=f32), "fw": fw, "kind": kind.astype(ml_dtypes.bfloat16),
        "final_norm": np.ascontiguousarray(final_norm, dtype=f32).reshape(1, D),
        "pool_w": np.ascontiguousarray(pool_w, dtype=f32), "w_kv": np.ascontiguousarray(w_kv, dtype=f32),
        "w_q": np.ascontiguousarray(w_q, dtype=f32), "w_o": np.ascontiguousarray(w_o, dtype=f32),
        "rel_bias": np.ascontiguousarray(rel_bias, dtype=f32), "gmat": gmat,
        "w_gate_up": np.ascontiguousarray(w_gate_up, dtype=f32),
        "w_down": np.ascontiguousarray(w_down, dtype=f32),
    }
    in_maps = []
    shared = {k: v for k, v in shared.items() if k in used}
    for b in range(NB):
        m = dict(shared)
        m["x"] = x[b]
        in_maps.append(m)
    res = run_bass_kernel_spmd(nc, in_maps, core_ids=list(range(NB)))
    for k in dbgs:
        LAST[k] = [np.asarray(res.results[b][k]) for b in range(NB)]
    out = np.stack([np.asarray(res.results[b]["out"]) for b in range(NB)], axis=0)
    return out.astype(f32)
```

```python
import numpy as np
import concourse.bass as bass
import concourse.mybir as mybir
from concourse.bass_utils import run_bass_kernel_spmd

F32 = mybir.dt.float32
BF16 = mybir.dt.bfloat16
AF = mybir.ActivationFunctionType
ALU = mybir.AluOpType
AX = mybir.AxisListType

D = 1024
T = 2048
NB = 8
DFF = 2816
NCH = 8
EPS = 1e-6
N_A = 2
DEPTH = 4
HD = 64
NH = 16

COMPUTE = ("pe", "act", "dve", "pool")
DMAQ_SLOTS = 8


class Op:
    __slots__ = ("eng", "fn", "dma", "deps", "signal", "sigval", "dslot", "dval", "idx", "qi")

    def __init__(self, eng, fn, dma, idx):
        self.eng = eng
        self.fn = fn
        self.dma = dma
        self.deps = []
        self.signal = False
        self.sigval = 0
        self.dslot = 0
        self.dval = 0
        self.idx = idx
        self.qi = 0


class Prog:
    def __init__(self, nc, strict=True):
        self.nc = nc
        self.ops = []
        self.last_w = {}
        self.readers = {}
        self.strict = strict
        self.qcount = {}

    def add(self, eng, fn, reads=(), writes=(), dma=False):
        op = Op(eng, fn, dma, len(self.ops))
        deps = {}

        def need(o):
            if o is op:
                return
            key = o if o.dma else o.eng
            cur = deps.get(key)
            if cur is None or cur.idx < o.idx:
                deps[key] = o

        for t in reads:
            w = self.last_w.get(t)
            if w is not None:
                need(w)
        for t in writes:
            w = self.last_w.get(t)
            if w is not None:
                need(w)
            for r in self.readers.get(t, {}).values():
                need(r)
        for t in reads:
            self.readers.setdefault(t, {})[op if dma else eng] = op
        for t in writes:
            self.last_w[t] = op
            self.readers[t] = {}
        if dma:
            qi = self.qcount.get(eng, 0)
            self.qcount[eng] = qi + 1
            op.qi = qi
            op.dslot = qi % DMAQ_SLOTS
            op.dval = 16 * (qi // DMAQ_SLOTS + 1)
        op.deps = list(deps.values())
        self.ops.append(op)
        return op

    def emit(self):
        nc = self.nc
        ops = self.ops
        prev_slot = {}
        for op in ops:
            if op.dma:
                k = (op.eng, op.dslot)
                p = prev_slot.get(k)
                if p is not None:
                    op.deps.append(p)
                prev_slot[k] = op
        for op in ops:
            for d in op.deps:
                if not d.dma and (d.eng != op.eng or op.dma or (self.strict and op.eng != "pe")):
                    d.signal = True
        cnt = {}
        for op in ops:
            if not op.dma and op.signal:
                cnt[op.eng] = cnt.get(op.eng, 0) + 1
                op.sigval = cnt[op.eng]
        by_eng = {}
        for op in ops:
            by_eng.setdefault(op.eng, []).append(op)

        import contextlib
        with contextlib.ExitStack() as st:
            esem = {e: st.enter_context(nc.semaphore("s_" + e)) for e in COMPUTE}
            dsem = {}
            for q in self.qcount:
                for s in range(DMAQ_SLOTS):
                    dsem[(q, s)] = st.enter_context(nc.semaphore("d_%s_%d" % (q, s)))
            block = st.enter_context(nc.Block())

            def run(ename, engobj):
                waited = {}
                for op in by_eng.get(ename, []):
                    for d in op.deps:
                        if d.dma:
                            sem, val = dsem[(d.eng, d.dslot)], d.dval
                        else:
                            if d.eng == ename and not (op.dma or (self.strict and ename != "pe")):
                                continue
                            sem, val = esem[d.eng], d.sigval
                        key = id(sem)
                        if waited.get(key, 0) >= val:
                            continue
                        waited[key] = val
                        engobj.wait_ge(sem, val)
                    ins = op.fn(engobj)
                    if op.dma:
                        ins.then_inc(dsem[(ename, op.dslot)], 16)
                    elif op.signal:
                        ins.then_inc(esem[ename], 1)
                last = {}
                for op in by_eng.get(ename, []):
                    if op.dma:
                        last[op.dslot] = op.dval
                for s, v in last.items():
                    engobj.wait_ge(dsem[(ename, s)], v)

            @block.tensor
            def _(e):
                run("pe", e)

            @block.scalar
            def _(e):
                run("act", e)

            @block.vector
            def _(e):
                run("dve", e)

            @block.gpsimd
            def _(e):
                run("pool", e)

            @block.sync
            def _(e):
                run("sp", e)


class Alloc:
    def __init__(self, nc, base=16512, limit=229344):
        self.nc = nc
        self.off = base
        self.limit = limit
        self.n = 0

    def t(self, name, shape, dtype):
        sz = 1
        for s in shape[1:]:
            sz *= s
        sz *= 2 if dtype == BF16 else 4
        off = (self.off + 63) // 64 * 64
        self.n += 1
        h = self.nc.alloc_sbuf_tensor_at("%s_%d" % (name, self.n), list(shape), dtype, offset=off)
        self.off = off + sz
        assert self.off <= self.limit, (name, self.off, self.limit)
        return h

    def fork(self):
        a = Alloc(self.nc, self.off, self.limit)
        a.n = self.n + 1000
        return a


NV = 88
DEBUG = False
ATT_STAGE = 9
ATT_HP = 8
ATT_OWN = 8
HACK = 0
USED = {}
LAST = {}


def NM(l):
    return l * 8


def NF(l):
    return 32 + l * 8


def PSC(l):
    return 64 + l * 8


KVO = 80


def tl(j, n=512):
    return slice(j * n, (j + 1) * n)


class Builder:
    def __init__(self, nc, layers, final=True):
        self.nc = nc
        self.P = Prog(nc)
        self.layers = layers
        self.final = final

    def pe(self, fn, r, w):
        return self.P.add("pe", fn, list(r) + ["ARENA"], w)

    def act(self, fn, r, w):
        return self.P.add("act", fn, list(r) + ["ARENA"], w)

    def dve(self, fn, r, w):
        return self.P.add("dve", fn, list(r) + ["ARENA"], w)

    def pool(self, fn, r, w):
        return self.P.add("pool", fn, list(r) + ["ARENA"], w)

    def dma(self, q, out, in_, r, w):
        return self.P.add(q, lambda e: e.dma_start(out=out, in_=in_), list(r) + ["ARENA"], w, dma=True)

    def barrier(self):
        self.P.add("pool", lambda e: e.memset(self.junk[:], 0.0), [], ["ARENA"])

    def dbg(self, name, ap, shape, reads, dt=F32):
        if not DEBUG:
            return
        d = self.nc.dram_tensor("dbg_" + name, list(shape), dt, kind="ExternalOutput").ap()
        self.dbgs["dbg_" + name] = d
        self.dma("sp", d, ap, reads, [("dbg", name)])

    def mm(self, out, lhsT, rhs, start, stop, r, w):
        return self.pe(lambda e: e.matmul(out, lhsT, rhs, start=start, stop=stop), r, w)

    def wload(self, src, stg, stok, wb, btok):
        self.dma("sp", stg, src, [], [stok])
        self.act(lambda e: e.activation(out=wb, in_=stg, func=AF.Copy), [stok], [btok])

    def build(self):
        nc = self.nc
        specs = {"x": [T, D], "vecs": [128, NV], "ident_in": [128, 128], "fw": [128, 64], "final_norm": [1, D],
                 "pool_w": [2, 4, 256, 256], "w_kv": [D, 2 * D], "w_q": [2, D, D], "w_o": [2, D, D],
                 "rel_bias": [32, 16], "gmat": [128, 16, 256], "w_gate_up": [DEPTH, D, 2 * DFF],
                 "w_down": [DEPTH, DFF, D], "kind": [128, T]}

        class Lazy(dict):
            def __missing__(d, name):
                d[name] = nc.dram_tensor(name, list(specs[name]), BF16 if name == "kind" else F32,
                                         kind="ExternalInput").ap()
                return d[name]

        dr = Lazy()
        dr["out"] = nc.dram_tensor("out", [T, D], F32, kind="ExternalOutput").ap()
        dr["kt_d"] = nc.dram_tensor("kt_d", [8, 128, T], BF16).ap()
        dr["v_d"] = nc.dram_tensor("v_d", [8, 128, 16 * 132], BF16).ap()
        self.dr = dr
        self.dbgs = {}

        A = Alloc(nc)
        self.XT = A.t("XT", [128, NCH, T], F32)
        self.ident = A.t("ident", [128, 128], F32)
        self.identb = A.t("identb", [128, 128], BF16)
        self.vecs = A.t("vecs", [128, NV], F32)
        self.epsc = A.t("epsc", [128, 1], F32)
        self.onesb = A.t("onesb", [128, 128], BF16)
        self.RS = A.t("RS", [128, T], F32)
        self.KMBD = A.t("KMBD", [128, 2, 8, 16], BF16)
        self.nb31 = A.t("nb31", [128, 16], F32)
        self.FW = A.t("FW", [128, 64], F32)
        self.junk = A.t("junk", [128, 1], F32)
        self.sqb = A.t("sqb", [128, NCH, 512], BF16)
        self.ps = [nc.alloc_psum_tensor("ps%d" % i, [128, 512], F32) for i in range(8)]

        self.dma("sp", self.ident[:], dr["ident_in"], [], ["ident"])
        self.dma("sp", self.vecs[:], dr["vecs"], [], ["vecs"])
        self.dma("sp", self.FW[:], dr["fw"], [], ["FW"])
        self.dma("sp", self.nb31[:], dr["rel_bias"][31:32, :].partition_broadcast(128).rearrange("p o d -> p (o d)"),
                 [], ["nb31"])
        self.dve(lambda e: e.tensor_scalar(out=self.nb31[:], in0=self.nb31[:], scalar1=-1.0, scalar2=None,
                                           op0=ALU.mult), ["nb31"], ["nb31"])
        self.pool(lambda e: e.memset(self.epsc[:], EPS), [], ["epsc"])
        self.pool(lambda e: e.memset(self.onesb[:], 1.0 / D), [], ["onesb"])
        self.act(lambda e: e.activation(out=self.identb[:], in_=self.ident[:], func=AF.Copy), ["ident"], ["identb"])

        self.load_x(A.fork())
        self.rs_valid = False

        def need_rs():
            if not self.rs_valid:
                self.rs_tiles()
            self.rs_valid = False

        nl = len(self.layers)
        for li, ll in enumerate(self.layers):
            l, part = (ll, "mf") if isinstance(ll, int) else ll
            if "m" in part:
                need_rs()
                if l < N_A:
                    self.pool_mixer(l, A.fork(), feed="f" in part)
                else:
                    if l == N_A:
                        self.kv_phase(A.fork())
                    self.attention(l, A.fork(), feed="f" in part)
                self.rs_valid = "f" in part
            if "k" in part:
                need_rs()
                self.kv_phase(A.fork())
                self.rs_valid = True
            if "a" in part:
                need_rs()
                self.attention(l, A.fork())
            if "f" in part:
                need_rs()
                fd = li + 1 < nl
                self.ffn(l, A.fork(), feed=fd)
                self.rs_valid = fd
        self.final_out(A.fork())
        self.P.emit()

    def load_x(self, A):
        self.barrier()
        xin = [A.t("xin", [128, D], F32) for _ in range(3)]
        xv = self.dr["x"].rearrange("(n p) d -> n p d", p=128)
        for n in range(T // 128):
            b = xin[n % 3]
            tk = ("xin", n % 3)
            self.dma("sp", b[:], xv[n], [], [tk])
            for half in range(2):
                pi = (n * 2 + half) % 4
                pt = self.ps[pi]
                ptk = ("ps", pi)
                for cc in range(4):
                    c = half * 4 + cc
                    self.pe(lambda e, pt=pt, b=b, c=c, cc=cc: e.transpose(
                        pt[:, cc * 128:(cc + 1) * 128], b[:, c * 128:(c + 1) * 128], self.ident[:]),
                        [tk, "ident"], [ptk])
                outap = self.XT[:, half * 4:(half + 1) * 4, n * 128:(n + 1) * 128]
                inap = pt[:].rearrange("p (c t) -> p c t", c=4)
                wt = [("XT", half * 4 + cc, n // 4) for cc in range(4)]
                if (n + half) % 2 == 0:
                    self.dve(lambda e, o=outap, i=inap: e.tensor_copy(out=o, in_=i), [ptk], wt)
                else:
                    self.act(lambda e, o=outap, i=inap: e.activation(out=o, in_=i, func=AF.Copy), [ptk], wt)

    def final_out(self, A):
        self.barrier()
        gfin = A.t("gfin", [128, D], F32)
        self.dma("sp", gfin[:], self.dr["final_norm"].partition_broadcast(128).rearrange("p o d -> p (o d)"),
                 [], ["gfin"])
        ob = [A.t("ob", [128, D], F32) for _ in range(2)]
        sq = A.t("sqjunk", [128, 512], F32)
        ssum = [A.t("ssum", [128, 2], F32) for _ in range(2)]
        ov = self.dr["out"].rearrange("(n p) d -> n p d", p=128)
        for n in range(T // 128):
            pts = [self.ps[(n % 2) * 2 + i] for i in range(2)]
            ptk = [("ps", (n % 2) * 2 + i) for i in range(2)]
            for c in range(NCH):
                half = c // 4
                self.pe(lambda e, pt=pts[half], c=c, n=n: e.transpose(
                    pt[:, (c % 4) * 128:(c % 4 + 1) * 128], self.XT[:, c, n * 128:(n + 1) * 128], self.ident[:]),
                    [("XT", c, n // 4), "ident"], [ptk[half]])
            s = ssum[n % 2]
            stk = ("ssum", n % 2)
            o = ob[n % 2]
            otk = ("ob", n % 2)
            for half in range(2):
                self.act(lambda e, pt=pts[half], s=s, half=half: e.activation(
                    out=sq[:], in_=pt[:], func=AF.Square, accum_out=s[:, half:half + 1]),
                    [ptk[half]], [stk, "sqjunk"])
            self.dve(lambda e, s=s: e.tensor_tensor(out=s[:, 0:1], in0=s[:, 0:1], in1=s[:, 1:2], op=ALU.add),
                     [stk], [stk])
            self.act(lambda e, s=s: e.activation(out=s[:, 0:1], in_=s[:, 0:1], func=AF.Sqrt,
                                                 bias=self.epsc[:], scale=1.0 / D), [stk, "epsc"], [stk])
            self.dve(lambda e, s=s: e.reciprocal(out=s[:, 0:1], in_=s[:, 0:1]), [stk], [stk])
            for half in range(2):
                self.dve(lambda e, pt=pts[half], s=s, o=o, half=half: e.scalar_tensor_tensor(
                    out=o[:, half * 512:(half + 1) * 512], in0=pt[:], scalar=s[:, 0:1],
                    in1=gfin[:, half * 512:(half + 1) * 512], op0=ALU.mult, op1=ALU.mult),
                    [ptk[half], stk, "gfin"], [otk])
            self.dma("sp", ov[n], o[:], [otk], [("out", n)])

    def rs_tiles(self, A=None):
        for j in range(4):
            b = self.sqb
            for c in range(NCH):
                xa = self.XT[:, c, tl(j)]
                if c in (0, 3, 6):
                    self.act(lambda e, o=b[:, c, :], xa=xa: e.activation(out=o, in_=xa, func=AF.Square),
                             [("XT", c, j)], [("sqb", c)])
                else:
                    fn = lambda e, o=b[:, c, :], xa=xa: e.tensor_tensor(out=o, in0=xa, in1=xa, op=ALU.mult)
                    (self.pool if c in (1, 5) else self.dve)(fn, [("XT", c, j)], [("sqb", c)])
            pb = 6 + (j % 2)
            for c in range(NCH):
                self.mm(self.ps[pb][:], self.onesb[:], b[:, c, :], c == 0, c == NCH - 1,
                        [("sqb", c), "onesb"], [("ps", pb)])
            self.act(lambda e, pb=pb, j=j: e.activation(out=self.RS[:, tl(j)], in_=self.ps[pb][:], func=AF.Ln,
                                                        bias=self.epsc[:], scale=1.0),
                     [("ps", pb), "epsc"], [("RS", j)])
            self.act(lambda e, j=j: e.activation(out=self.RS[:, tl(j)], in_=self.RS[:, tl(j)], func=AF.Exp,
                                                 scale=-0.5), [("RS", j)], [("RS", j)])

    def rs_feeder(self, banks):
        B = self
        st = {"k": 0, "pend": []}

        def square(c, j):
            slot = st["k"] % NCH
            st["k"] += 1
            o = B.sqb[:, slot, :]
            xa = B.XT[:, c, tl(j)]
            B.act(lambda e, o=o, xa=xa: e.activation(out=o, in_=xa, func=AF.Square), [("XT", c, j)], [("sqb", slot)])
            st["pend"].append((c, j, slot))

        def flush(keep=0):
            while len(st["pend"]) > keep:
                c, j, slot = st["pend"].pop(0)
                bank = banks[j]
                B.mm(B.ps[bank][:], B.onesb[:], B.sqb[:, slot, :], c == 0, c == NCH - 1,
                     [("sqb", slot), "onesb"], [("ps", bank)])
                if c == NCH - 1:
                    B.act(lambda e, bank=bank, j=j: e.activation(out=B.RS[:, tl(j)], in_=B.ps[bank][:], func=AF.Ln,
                                                                 bias=B.epsc[:], scale=1.0),
                          [("ps", bank), "epsc"], [("RS", j)])
                    B.act(lambda e, j=j: e.activation(out=B.RS[:, tl(j)], in_=B.RS[:, tl(j)], func=AF.Exp,
                                                      scale=-0.5), [("RS", j)], [("RS", j)])
        return square, flush

    def make_hT(self, HT, name, goff, j, jj):
        for c in range(NCH):
            self.dve(lambda e, c=c: e.scalar_tensor_tensor(
                out=HT[:, c, tl(jj)], in0=self.XT[:, c, tl(j)], scalar=self.vecs[:, goff + c:goff + c + 1],
                in1=self.RS[:, tl(j)], op0=ALU.mult, op1=ALU.mult),
                [("XT", c, j), ("RS", j), "vecs"], [(name, c, jj)])

    def ffn(self, l, A, feed=False):
        self.barrier()
        fsq, fflush = self.rs_feeder({0: 6, 1: 7, 2: 6, 3: 7})
        dr = self.dr
        HT = A.t("HTF", [128, NCH, 1024], BF16)
        ACTT = A.t("ACTT", [128, 22, 1024], BF16)
        stg = [A.t("gstg", [128, NCH, 256], F32) for _ in range(2)]
        wb = [A.t("gwb", [128, NCH, 256], BF16) for _ in range(4)]
        dstg = A.t("dstg", [128, 22, 128], F32)
        dwb = [A.t("dwb", [128, 22, 128], BF16) for _ in range(2)]
        sig = [A.t("sig", [128, 512], F32) for _ in range(2)]
        wgu = dr["w_gate_up"][l].rearrange("(c p) f -> p c f", p=128)
        wd = dr["w_down"][l].rearrange("(k p) d -> p k d", p=128)
        units = []
        for g in range(11):
            units += [(g, 0), (g, 1)]

        def load_unit(i):
            g, gu = units[i]
            col = gu * DFF + g * 256
            self.wload(wgu[:, :, col:col + 256], stg[i % 2][:], ("gstg", i % 2), wb[i % 4][:], ("gwb", i % 4))

        def load_d(oc):
            self.wload(wd[:, :, oc * 128:(oc + 1) * 128], dstg[:], "dstg", dwb[oc % 2][:], ("dwb", oc % 2))

        load_unit(0)
        load_unit(1)
        for jj in range(2):
            self.make_hT(HT, "HTF", NF(l), jj, jj)
        for s in range(2):
            cnt = 0
            for g in range(11):
                if g == 8:
                    load_d(0)
                for k in (2 * g + 2, 2 * g + 3):
                    if k < 22:
                        load_unit(k)
                ig, iu = 2 * g, 2 * g + 1
                for fc in range(2):
                    f = 2 * g + fc
                    for tt in range(2):
                        r = cnt % 2
                        cnt += 1
                        pg, pu = r, 2 + r
                        for (pbk, iw) in ((pg, ig), (pu, iu)):
                            for c in range(NCH):
                                self.mm(self.ps[pbk][:], wb[iw % 4][:, c, fc * 128:(fc + 1) * 128], HT[:, c, tl(tt)],
                                        c == 0, c == NCH - 1, [("gwb", iw % 4), ("HTF", c, tt)], [("ps", pbk)])
                        self.act(lambda e, r=r, pg=pg: e.activation(out=sig[r][:], in_=self.ps[pg][:], func=AF.Silu),
                                 [("ps", pg)], [("sig", r)])
                        self.dve(lambda e, r=r, pu=pu, f=f, tt=tt: e.tensor_tensor(
                            out=ACTT[:, f, tl(tt)], in0=sig[r][:], in1=self.ps[pu][:], op=ALU.mult),
                            [("sig", r), ("ps", pu)], [("ACTT", f, tt)])

            if s == 0:
                load_unit(0)
                load_unit(1)
                for jj in range(2):
                    self.make_hT(HT, "HTF", NF(l), 2 + jj, jj)
            for oc in range(8):
                if oc + 1 < 8:
                    load_d(oc + 1)
                for tt in range(2):
                    pb = 4 + (oc * 2 + tt) % 2
                    j = 2 * s + tt
                    for k in range(22):
                        self.mm(self.ps[pb][:], dwb[oc % 2][:, k, :], ACTT[:, k, tl(tt)], k == 0, k == 21,
                                [("dwb", oc % 2), ("ACTT", k, tt)], [("ps", pb)])
                    if feed:
                        fflush(0)
                    self.dve(lambda e, pb=pb, oc=oc, j=j: e.tensor_tensor(
                        out=self.XT[:, oc, tl(j)], in0=self.ps[pb][:], in1=self.XT[:, oc, tl(j)], op=ALU.add),
                        [("ps", pb), ("XT", oc, j)], [("XT", oc, j)])
                    if feed:
                        fsq(oc, j)
            if feed:
                fflush(0)

    def pool_mixer(self, l, A, feed=False):
        self.barrier()
        fsq, fflush = self.rs_feeder({0: 4, 1: 5, 2: 6, 3: 7})
        dr = self.dr
        HCs = [A.t("HC", [128, 16 + T], F32) for _ in range(2)]
        Ss = [[A.t("SW", [128, 16 + T], F32) for _ in range(2)] for _ in range(2)]
        DT = [A.t("DT", [128, 2, T], BF16) for _ in range(2)]
        pstg = A.t("pstg", [128, 8, 256], F32)
        PW = A.t("PW", [128, 8, 256], BF16)
        tmp16s = [A.t("tmp16", [128, 16], F32) for _ in range(2)]
        self.wload(dr["pool_w"][l].rearrange("g (ci p) e -> p (g ci) e", p=128), pstg[:], "pstg", PW[:], "PW")
        for q in range(2):
            self.pool(lambda e, q=q: e.memset(HCs[q][:, 0:16], 0.0), [], [("HC", q)])
            for i in range(2):
                self.pool(lambda e, q=q, i=i: e.memset(Ss[q][i][:, 0:16], 0.0), [], [("SW", q, i)])
        allrs = [("RS", j) for j in range(4)]

        def emit_HC(g, ci):
            c = 2 * g + ci
            HC, hctk = HCs[ci], ("HC", ci)
            self.dve(lambda e, c=c, HC=HC: e.scalar_tensor_tensor(
                out=HC[:, 16:], in0=self.XT[:, c, :], scalar=self.vecs[:, NM(l) + c:NM(l) + c + 1],
                in1=self.RS[:, :], op0=ALU.mult, op1=ALU.mult),
                [("XT", c, j) for j in range(4)] + allrs + ["vecs"], [hctk])

        def emit_adds(g, ci):
            w = 2 << g
            S = Ss[ci]
            src, stok = HCs[ci], ("HC", ci)
            sh = 1
            k = 0
            while sh < w:
                dst, dtok = S[k % 2], ("SW", ci, k % 2)
                fn = lambda e, dst=dst, src=src, sh=sh: e.tensor_tensor(
                    out=dst[:, 16:], in0=src[:, 16:], in1=src[:, 16 - sh:16 - sh + T], op=ALU.add)
                (self.dve if (g == 3 and sh * 2 >= w) else self.pool)(fn, [stok], [dtok])
                src, stok = dst, dtok
                sh *= 2
                k += 1
            return src, stok

        def emit_D(g, ci, src, stok):
            w = 2 << g
            dt = DT[g % 2]
            HC, hctk = HCs[ci], ("HC", ci)
            tmp16, t16tk = tmp16s[ci], ("tmp16", ci)
            self.dve(lambda e, src=src, ci=ci, dt=dt, w=w, HC=HC: e.scalar_tensor_tensor(
                out=dt[:, ci, :], in0=src[:, 16:], scalar=1.0 / w, in1=HC[:, 16:], op0=ALU.mult,
                op1=ALU.subtract), [stok, hctk], [("DT", g % 2, ci)])
            self.dve(lambda e, src=src, g=g, tmp16=tmp16: e.tensor_tensor(
                out=tmp16[:], in0=src[:, 16:32], in1=self.FW[:, g * 16:(g + 1) * 16], op=ALU.mult),
                [stok, "FW"], [t16tk])
            self.dve(lambda e, ci=ci, dt=dt, tmp16=tmp16, HC=HC: e.tensor_tensor(
                out=dt[:, ci, 0:16], in0=tmp16[:], in1=HC[:, 16:32], op=ALU.subtract),
                [t16tk, hctk], [("DT", g % 2, ci)])

        def emit_mm(g):
            dt = DT[g % 2]
            for eo in range(2):
                co = 2 * g + eo
                for j in range(4):
                    pb = (eo * 4 + j) % 4
                    for ci in range(2):
                        self.mm(self.ps[pb][:], PW[:, 2 * g + ci, eo * 128:(eo + 1) * 128], dt[:, ci, tl(j)],
                                ci == 0, ci == 1, ["PW", ("DT", g % 2, ci)], [("ps", pb)])
                    if feed:
                        fflush(2)
                    self.dve(lambda e, pb=pb, co=co, j=j: e.scalar_tensor_tensor(
                        out=self.XT[:, co, tl(j)], in0=self.ps[pb][:],
                        scalar=self.vecs[:, PSC(l) + co:PSC(l) + co + 1], in1=self.XT[:, co, tl(j)],
                        op0=ALU.mult, op1=ALU.add), [("ps", pb), ("XT", co, j), "vecs"], [("XT", co, j)])
                    if feed:
                        fsq(co, j)

        emit_HC(0, 0)
        emit_HC(0, 1)
        for g in range(4):
            r0 = emit_adds(g, 0)
            r1 = emit_adds(g, 1)
            emit_D(g, 0, *r0)
            if g < 3:
                emit_HC(g + 1, 0)
            emit_D(g, 1, *r1)
            if g < 3:
                emit_HC(g + 1, 1)
            emit_mm(g)
        if feed:
            fflush(0)

    def kv_phase(self, A):
        self.barrier()
        dr = self.dr
        HT = A.t("HTK", [128, NCH, T], BF16)
        kstg = [A.t("kstg", [128, NCH, 128], F32) for _ in range(2)]
        kwb = [A.t("kwb", [128, NCH, 128], BF16) for _ in range(3)]
        KTB = [A.t("KTB", [128, T], BF16) for _ in range(2)]
        VTS = [A.t("VTS", [128, T], BF16) for _ in range(2)]
        VB = [A.t("VB", [128, 16, 132], BF16) for _ in range(2)]
        kmf = A.t("kmf", [128, 8], F32)
        wkv = dr["w_kv"].rearrange("(c p) f -> p c f", p=128)
        units = []
        for hp in range(ATT_HP):
            units += [(hp, 0), (hp, 1)]

        def load(i):
            hp, kv = units[i]
            col = kv * D + hp * 128
            self.wload(wkv[:, :, col:col + 128], kstg[i % 2][:], ("kstg", i % 2), kwb[i % 3][:], ("kwb", i % 3))

        load(0)
        load(1)
        for jx in range(4):
            self.make_hT(HT, "HTK", KVO, jx, jx)
        for i in range(2):
            self.pool(lambda e, i=i: e.memset(VB[i][:], 1.0), [], [("VB", i)])
        self.pool(lambda e: e.memset(self.KMBD[:], 0.0), [], ["KMBD"])
        nb = 0
        for i, (hp, kv) in enumerate(units):
            if i + 2 < len(units):
                load(i + 2)
            w = kwb[i % 3]
            wtk = ("kwb", i % 3)
            if kv == 0:
                ktb = KTB[hp % 2]
                for jx in range(4):
                    pb = nb % 4
                    nb += 1
                    for c in range(NCH):
                        self.mm(self.ps[pb][:], w[:, c, :], HT[:, c, tl(jx)], c == 0, c == NCH - 1,
                                [wtk, ("HTK", c, jx)], [("ps", pb)])
                    self.act(lambda e, pb=pb, jx=jx, ktb=ktb: e.activation(out=ktb[:, tl(jx)], in_=self.ps[pb][:],
                                                                            func=AF.Copy),
                             [("ps", pb)], [("KTB", hp % 2)])
                self.dma("sp", dr["kt_d"][hp], ktb[:], [("KTB", hp % 2)], [("kt_d", hp)])
                self.dve(lambda e, ktb=ktb: e.tensor_reduce(
                    out=kmf[:, 0:8], in_=ktb[:].rearrange("p (n k) -> p n k", k=256), axis=AX.X, op=ALU.add),
                    [("KTB", hp % 2)], ["kmf"])
                self.dve(lambda e, hp=hp: e.tensor_scalar(out=self.KMBD[0:64, 0, hp, 0:8], in0=kmf[0:64, :],
                                                          scalar1=1.0 / 256, scalar2=None, op0=ALU.mult),
                         ["kmf"], ["KMBD"])
                self.dve(lambda e, hp=hp: e.tensor_scalar(out=self.KMBD[64:128, 1, hp, 8:16], in0=kmf[64:128, :],
                                                          scalar1=1.0 / 256, scalar2=None, op0=ALU.mult),
                         ["kmf"], ["KMBD"])
            else:
                vts = VTS[hp % 2]
                vb = VB[hp % 2]
                for jx in range(4):
                    pb = nb % 4
                    nb += 1
                    for c in range(NCH):
                        self.mm(self.ps[pb][:], w[:, c, :], HT[:, c, tl(jx)], c == 0, c == NCH - 1,
                                [wtk, ("HTK", c, jx)], [("ps", pb)])
                    self.dve(lambda e, pb=pb, jx=jx, vts=vts: e.tensor_copy(out=vts[:, tl(jx)], in_=self.ps[pb][:]),
                             [("ps", pb)], [("VTS", hp % 2, jx)])
                for half in range(2):
                    bank = 4 + half
                    psb = self.ps[bank][:].bitcast(BF16)
                    for t8 in range(8):
                        tk = half * 8 + t8
                        self.pe(lambda e, psb=psb, t8=t8, tk=tk, vts=vts: e.transpose(
                            psb[:, t8 * 128:(t8 + 1) * 128], vts[:, tk * 128:(tk + 1) * 128], self.identb[:]),
                            [("VTS", hp % 2, tk // 4), "identb"], [("ps", bank)])
                    oap = vb[:, half * 8:(half + 1) * 8, :].rearrange("p t (h e) -> p t h e", e=66)[:, :, :, 0:64]
                    iap = psb[:, 0:1024].rearrange("p (t h e) -> p t h e", t=8, h=2)
                    if half == 0:
                        self.act(lambda e, oap=oap, iap=iap: e.activation(out=oap, in_=iap, func=AF.Copy),
                                 [("ps", bank)], [("VB", hp % 2)])
                    else:
                        self.dve(lambda e, oap=oap, iap=iap: e.tensor_copy(out=oap, in_=iap),
                                 [("ps", bank)], [("VB", hp % 2)])
                for hf in range(2):
                    self.dma("sp", dr["v_d"][hp][:, hf * 1056:(hf + 1) * 1056],
                             vb[:, hf * 8:(hf + 1) * 8, :].rearrange("p t e -> p (t e)"), [("VB", hp % 2)],
                             [("v_d", hp, hf)])

    def attention(self, l, A, feed=False):
        self.barrier()
        fsq, fflush = self.rs_feeder({0: 0, 1: 1, 2: 2, 3: 3})
        dr = self.dr
        jl = l - N_A
        PTb = [A.t("PT", [128, 8, 2, 256], BF16) for _ in range(2)]
        PF = A.t("PF", [128, 512], F32)
        PFS = A.t("PFS", [128, 128], F32)
        ON = [A.t("ON", [128, 2, 128], BF16) for _ in range(2)]
        HT = A.t("HTQ", [128, NCH, T], BF16)
        OT = A.t("OT", [128, NCH, T], BF16)
        KTv = [A.t("KTA", [128, T], BF16), A.t("KTB", [128, T], BF16)]
        QTa = [A.t("QT", [128, T], BF16) for _ in range(2)]
        QTb = [[QTa[0][:], QTa[1][:]],
               [self.sqb[:, 0:4, :].rearrange("p c t -> p (c t)"), self.sqb[:, 4:8, :].rearrange("p c t -> p (c t)")]]
        V = [A.t("V", [128, 16, 132], BF16) for _ in range(2)]
        qstg = A.t("qstg", [128, NCH, 64], F32)
        GST = A.t("GST", [128, 2, 256], F32)
        qwb = [A.t("qwb", [128, NCH, 128], BF16) for _ in range(2)]
        EG = A.t("EG", [128, 2, 2, 256], F32)
        RINV = [A.t("RINV", [128, 2], F32) for _ in range(2)]
        GT = A.t("GT", [128, 16, 8], F32)
        GT2 = A.t("GT2", [128, 16, 8], F32)
        MK = A.t("MK", [128, 16, 8], F32)
        MX = A.t("MX", [128, 16], F32)
        SEL = A.t("SEL", [128, 16, 8], F32)
        INV = A.t("INV", [128, 16, 8], F32)
        M128 = A.t("M128", [128, 4, 2, 128], BF16)
        for j in range(4):
            self.make_hT(HT, "HTQ", NM(l), j, j)
        self.pool(lambda e: e.memset(EG[:, :, 1, 0:128], 0.0), [], ["EG"])
        for b in range(2):
            self.pool(lambda e, b=b: e.memset(QTb[b][0][64:128, :], 0.0), [("sqb", c) for c in range(NCH)],
                      [("QT", b, o) for o in range(8)] + [("sqb", c) for c in range(NCH)])
            self.pool(lambda e, b=b: e.memset(QTb[b][1][0:64, :], 0.0), [], [("QT", b, o) for o in range(8)])
        self.pool(lambda e: e.memset(M128[:], 0.0), [], ["M128"])
        self.pool(lambda e: e.memset(INV[:], 0.0), [], ["INV"])
        for o in range(4, 8):
            self.pool(lambda e, o=o: e.memset(INV[:, (o - 4) * 4:(o - 4) * 4 + 4, o:8], 1.0), [], ["INV"])
        self.dma("sp", KTv[0][64:128, :], dr["kind"][64:128, :], [], ["KTA"])
        self.dma("sp", KTv[1][0:64, :], dr["kind"][0:64, :], [], ["KTB"])
        wq = dr["w_q"][jl].rearrange("(c p) f -> p c f", p=128)
        wo = dr["w_o"][jl].rearrange("(c p) f -> p c f", p=128)
        ps7b = self.ps[7][:].bitcast(BF16)
        ps6b = self.ps[6][:].bitcast(BF16)
        cnt = {"sb": 0, "slot": 0, "pf": 0, "unit": 0}
        for hp in range(ATT_HP):
            v = V[hp % 2]
            vtk = ("V", hp % 2)
            self.dma("sp", KTv[0][0:64, :], dr["kt_d"][hp][0:64, :], [("kt_d", hp)], ["KTA"])
            self.dma("sp", KTv[1][64:128, :], dr["kt_d"][hp][64:128, :], [("kt_d", hp)], ["KTB"])
            for hf in range(2):
                self.dma("sp", v[:, hf * 8:(hf + 1) * 8, :].rearrange("p t e -> p (t e)"),
                         dr["v_d"][hp][:, hf * 1056:(hf + 1) * 1056], [("v_d", hp, hf)], [vtk])
            if hp == 0:
                self.dma("sp", GST[:], dr["gmat"][:, 0:2, :], [], ["GST"])
            for hh in range(2):
                h = 2 * hp + hh
                self.act(lambda e, hh=hh, h=h: e.activation(out=EG[:, hh, 0, :], in_=GST[:, hh, :], func=AF.Exp,
                                                            bias=self.nb31[:, h:h + 1], scale=1.0),
                         ["GST", "nb31"], ["EG"])
                self.act(lambda e, hh=hh: e.activation(out=EG[:, hh, 1, 128:256], in_=EG[:, hh, 0, 0:128],
                                                       func=AF.Copy), ["EG"], ["EG"])
            if hp + 1 < 8:
                self.dma("sp", GST[:], dr["gmat"][:, 2 * hp + 2:2 * hp + 4, :], [], ["GST"])

            def prologue(hp):
                QT = QTb[hp % 2]
                qb = hp % 2
                chunks = []

                def c0():
                    for hf in range(2):
                        self.wload(wq[:, :, hp * 128 + hf * 64:hp * 128 + (hf + 1) * 64], qstg[:], "qstg",
                                   qwb[hp % 2][:, :, hf * 64:(hf + 1) * 64], ("qwb", hp % 2))
                chunks.append(c0)

                def cj(j):
                    def f():
                        for c in range(NCH):
                            self.mm(self.ps[7][:], qwb[hp % 2][:, c, :], HT[:, c, tl(j)], c == 0, c == NCH - 1,
                                    [("qwb", hp % 2), ("HTQ", c, j)], [("ps", 7)])
                        for hq in range(2):
                            self.dve(lambda e, hq=hq: e.tensor_scalar(
                                out=QT[hq][64 * hq:64 * hq + 64, tl(j)], in0=self.ps[7][64 * hq:64 * hq + 64, :],
                                scalar1=0.125, scalar2=None, op0=ALU.mult), [("ps", 7)],
                                [("QT", qb, 2 * j), ("QT", qb, 2 * j + 1)])
                    return f
                for j in range(4):
                    chunks.append(cj(j))

                def cg():
                    for o in range(4, 8):
                        for qs in range(2):
                            col = (o - 4) * 32 + qs * 16
                            for hq in range(2):
                                self.mm(self.ps[7][:, col:col + 16],
                                        QT[hq][:, o * 256 + qs * 128:o * 256 + (qs + 1) * 128],
                                        self.KMBD[:, hq, hp, :], hq == 0, hq == 1, [("QT", qb, o), "KMBD"],
                                        [("ps", 7)])
                    self.pool(lambda e: e.memset(GT[:], -1e30), [], ["GT"])
                    for o in range(4, 8):
                        self.dve(lambda e, o=o: e.tensor_copy(
                            out=GT[:, (o - 4) * 4:(o - 4) * 4 + 4, 0:o],
                            in_=self.ps[7][:, (o - 4) * 32:(o - 4) * 32 + 32].rearrange(
                                "p (a n) -> p a n", n=8)[:, :, 0:o]), [("ps", 7)], ["GT"])
                    cur = GT
                    for it in range(3):
                        self.dve(lambda e, cur=cur: e.tensor_reduce(out=MX[:], in_=cur[:], axis=AX.X, op=ALU.max),
                                 ["GT"], ["GT"])
                        if it == 2:
                            break
                        self.dve(lambda e, cur=cur: e.tensor_tensor(
                            out=MK[:], in0=cur[:], in1=MX[:].unsqueeze(2).broadcast_to([128, 16, 8]), op=ALU.is_ge),
                            ["GT"], ["GT"])
                        self.dve(lambda e, cur=cur: e.scalar_tensor_tensor(
                            out=GT2[:], in0=MK[:], scalar=-1e30, in1=cur[:], op0=ALU.mult, op1=ALU.add),
                            ["GT"], ["GT"])
                        cur = GT2
                    self.dve(lambda e: e.tensor_tensor(
                        out=SEL[:], in0=GT[:], in1=MX[:].unsqueeze(2).broadcast_to([128, 16, 8]), op=ALU.is_ge),
                        ["GT"], ["SEL"])
                    self.dve(lambda e: e.tensor_tensor(out=SEL[:], in0=SEL[:], in1=INV[:], op=ALU.max),
                             ["SEL", "INV"], ["SEL"])
                    selv = SEL[:].rearrange("p (o q h) n -> p o q h n", o=4, q=2)
                    for hh in range(2):
                        base = 64 if hh == 0 else 0
                        self.dve(lambda e, hh=hh, base=base: e.tensor_scalar(
                            out=M128[:, :, :, base:base + 8], in0=selv[:, :, :, hh, :], scalar1=-1.0,
                            scalar2=30000.0, op0=ALU.add, op1=ALU.mult), ["SEL"], ["M128"])
                chunks.append(cg)
                return chunks

            if hp == 0:
                for ch in prologue(0):
                    ch()
            next_chunks = prologue(hp + 1) if hp + 1 < 8 else []
            QT = QTb[hp % 2]
            qb = hp % 2
            nsel = 4
            def emit_mask_rows():
                for o in range(4, 4 + nsel):
                    for qs in range(2):
                        cb = (o - 4) * 256 + qs * 128
                        self.pe(lambda e, o=o, qs=qs, cb=cb: e.transpose(ps7b[:, cb:cb + 128], M128[:, o - 4, qs, :],
                                                                         self.identb[:]),
                                ["M128", "identb"], [("ps", 7)])
                n = nsel * 256
                oa = QT[0][64:72, 1024:1024 + n]
                ob = QT[1][0:8, 1024:1024 + n]
                ia = ps7b[64:72, 0:n]
                ib = ps7b[0:8, 0:n]
                wtk = [("QT", qb, o) for o in range(4, 4 + nsel)]
                self.act(lambda e, oa=oa, ia=ia: e.activation(out=oa, in_=ia, func=AF.Copy), [("ps", 7)], wtk)
                self.act(lambda e, ob=ob, ib=ib: e.activation(out=ob, in_=ib, func=AF.Copy), [("ps", 7)], wtk)

            own_order = [0, 7, 1, 6, 2, 5, 3, 4]
            units = []
            for a, b in ((0, 7), (1, 6), (2, 5), (3, 4)):
                units += [(a, 0), (b, 0), (a, 1), (b, 1)]
            onidx = {own: k % 2 for k, own in enumerate(own_order)}

            def emit_block_scores(own, hh, n, pbi):
                q0 = own * 256
                PT = PTb[pbi]
                sb = cnt["sb"] % 3
                cnt["sb"] += 1
                for k2 in range(2):
                    kk = n * 256 + k2 * 128
                    self.mm(self.ps[sb][:, k2 * 256:(k2 + 1) * 256], KTv[hh][:, kk:kk + 128],
                            QT[hh][:, q0:q0 + 256], True, True, ["KTA" if hh == 0 else "KTB", ("QT", qb, own)],
                            [("ps", sb)])
                ptk = ("PT", pbi, n)
                if n == own:
                    self.act(lambda e, sb=sb: e.activation(out=PF[:], in_=self.ps[sb][:], func=AF.Exp),
                             [("ps", sb)], ["PF"])
                    self.dve(lambda e, PT=PT, n=n, hh=hh: e.tensor_tensor(
                        out=PT[:, n, :, :], in0=PF[:].rearrange("p (a b) -> p a b", a=2),
                        in1=EG[:, hh, :, :], op=ALU.mult), ["PF", "EG"], [ptk])
                elif n == own - 1:
                    self.act(lambda e, sb=sb, PT=PT, n=n: e.activation(
                        out=PT[:, n, :, :], in_=self.ps[sb][:].rearrange("p (a b) -> p a b", a=2),
                        func=AF.Exp), [("ps", sb)], [ptk])
                    self.act(lambda e, sb=sb: e.activation(out=PFS[:, 0:128], in_=self.ps[sb][:, 256:384],
                                                          func=AF.Exp), [("ps", sb)], ["PFS"])
                    self.dve(lambda e, PT=PT, n=n, hh=hh: e.tensor_tensor(
                        out=PT[:, n, 1, 0:128], in0=PFS[:, 0:128], in1=EG[:, hh, 0, 128:256], op=ALU.mult),
                        ["PFS", "EG"], [ptk])
                else:
                    self.act(lambda e, sb=sb, PT=PT, n=n: e.activation(
                        out=PT[:, n, :, :], in_=self.ps[sb][:].rearrange("p (a b) -> p a b", a=2),
                        func=AF.Exp), [("ps", sb)], [ptk])

            def emit_pv_block(u, n):
                own, hh, pbi, par = u
                PT = PTb[pbi]
                for qs in range(2):
                    bank = 3 + 2 * par + qs
                    for k2 in range(2):
                        if n == own and k2 == 1 and qs == 0:
                            continue
                        bo = border(own)
                        first = (n == bo[0] and k2 == 0)
                        last = (n == bo[-1] and k2 == (1 if (qs == 1 or n != own) else 0))
                        self.mm(self.ps[bank][:, 0:66], PT[:, n, k2, qs * 128:(qs + 1) * 128],
                                v[:, n * 2 + k2, hh * 66:(hh + 1) * 66], first, last,
                                [("PT", pbi, n), vtk], [("ps", bank)])

            def emit_pv_finish(u, hp=hp):
                own, hh, pbi, par = u
                on = ON[onidx[own]]
                ontk = ("ON", onidx[own])
                rinv = RINV[par]
                rtk = ("RINV", par)
                for qs in range(2):
                    bank = 3 + 2 * par + qs
                    self.dve(lambda e, bank=bank, qs=qs, rinv=rinv: e.reciprocal(
                        out=rinv[:, qs:qs + 1], in_=self.ps[bank][:, 64:65]), [("ps", bank)], [rtk])
                    self.dve(lambda e, bank=bank, qs=qs, rinv=rinv, on=on, hh=hh: e.tensor_scalar(
                        out=on[:, qs, hh * 64:(hh + 1) * 64], in0=self.ps[bank][:, 0:64], scalar1=rinv[:, qs:qs + 1],
                        scalar2=None, op0=ALU.mult), [("ps", bank), rtk], [ontk])
                if hh == 1:
                    q0 = own * 256

                    def tr(on=on, ontk=ontk, q0=q0, hp=hp, own=own):
                        for qs in range(2):
                            self.pe(lambda e, on=on, qs=qs: e.transpose(ps7b[:, qs * 128:(qs + 1) * 128],
                                                                         on[:, qs, :], self.identb[:]),
                                    [ontk, "identb"], [("ps", 7)])
                        self.dve(lambda e, q0=q0, hp=hp: e.tensor_copy(out=OT[:, hp, q0:q0 + 256],
                                                                       in_=ps7b[:, 0:256]),
                                 [("ps", 7)], [("OT", hp, own // 2)])
                    pending.append(tr)

            prev = None
            pending = []

            def border(own):
                return ([own] + ([own - 1] if own >= 1 else []) + list(range(0, own - 1)))
            emit_mask_rows()
            LAG = 2
            for (own, hh) in units:
                par = cnt["unit"] % 2
                cnt["unit"] += 1
                pbi = par
                order = border(own)
                porder = border(prev[0]) if prev is not None else []
                npv = 0
                for i, n in enumerate(order):
                    emit_block_scores(own, hh, n, pbi)
                    if i >= LAG and npv < len(porder):
                        emit_pv_block(prev, porder[npv])
                        npv += 1
                    if own >= 5 and i in (3, 5) and next_chunks:
                        next_chunks.pop(0)()
                if prev is not None:
                    for n in porder[npv:]:
                        emit_pv_block(prev, n)
                    if pending:
                        pending.pop(0)()
                    emit_pv_finish(prev)
                prev = (own, hh, pbi, par)
            if prev is not None:
                for n in border(prev[0]):
                    emit_pv_block(prev, n)
                emit_pv_finish(prev)
            while next_chunks:
                next_chunks.pop(0)()
            while pending:
                pending.pop(0)()
        def load_o(oc):
            for hf in range(2):
                self.wload(wo[:, :, oc * 128 + hf * 64:oc * 128 + (hf + 1) * 64], qstg[:], "qstg",
                           qwb[oc % 2][:, :, hf * 64:(hf + 1) * 64], ("qwb", oc % 2))

        load_o(0)
        for oc in range(8):
            if oc + 1 < 8:
                load_o(oc + 1)
            for j in range(4):
                pb = 6 + j % 2
                for c in range(NCH):
                    self.mm(self.ps[pb][:], qwb[oc % 2][:, c, :], OT[:, c, tl(j)], c == 0, c == NCH - 1,
                            [("qwb", oc % 2), ("OT", c, j)], [("ps", pb)])
                if feed:
                    fflush(1)
                self.dve(lambda e, pb=pb, oc=oc, j=j: e.tensor_tensor(
                    out=self.XT[:, oc, tl(j)], in0=self.ps[pb][:], in1=self.XT[:, oc, tl(j)], op=ALU.add),
                    [("ps", pb), ("XT", oc, j)], [("XT", oc, j)])
                if feed:
                    fsq(oc, j)
        if feed:
            fflush(0)
        self.barrier()


_CACHE = {}


def _build(layers, final=True):
    key = (tuple(layers), final, DEBUG, ATT_STAGE, ATT_HP, ATT_OWN, HACK)
    if key not in _CACHE:
        nc = bass.Bass("TRN2", target_bir_lowering=False)
        b = Builder(nc, layers, final)
        b.build()
        _CACHE[key] = (nc, set(k for k in b.dr.keys()), list(b.dbgs.keys()))
    return _CACHE[key]


def _rel_bucket(n):
    n = np.asarray(n)
    nf = np.maximum(n, 16).astype(np.float32)
    large = 16 + (np.log(nf / np.float32(16)) / np.float32(np.log(128 / 16)) * np.float32(16)).astype(np.int32)
    large = np.minimum(large, 31)
    return np.where(n < 16, n, large)


def _host_inputs(norm_mixer, norm_ffn, pool_scale, kv_norm, rel_bias):
    f32 = np.float32
    cols = [np.asarray(norm_mixer, f32).reshape(4, 8, 128), np.asarray(norm_ffn, f32).reshape(4, 8, 128),
            np.asarray(pool_scale, f32).reshape(2, 8, 128), np.asarray(kv_norm, f32).reshape(1, 8, 128)]
    vecs = np.concatenate([c.reshape(-1, 128) for c in cols], axis=0).T.copy()
    assert vecs.shape == (128, NV)
    fw = np.zeros((128, 64), f32)
    for g in range(4):
        w = 2 << g
        fw[:, g * 16:(g + 1) * 16] = 1.0 / np.minimum(np.arange(16) + 1, w)
    p = np.arange(128)[:, None]
    c = np.arange(256)[None, :]
    dist = c - p
    bucket = _rel_bucket(np.maximum(dist, 0))
    rb = np.asarray(rel_bias, f32)
    gm = rb[bucket]
    gm = np.where((dist >= 0)[:, :, None], gm, f32(-30000.0))
    gmat = np.ascontiguousarray(gm.transpose(0, 2, 1)).astype(f32)
    return vecs, fw, gmat


def kernel(x, norm_mixer, norm_ffn, pool_w, pool_scale, kv_norm, w_kv, w_q, w_o,
           rel_bias, w_gate_up, w_down, final_norm, _layers=None):
    layers = list(range(DEPTH)) if _layers is None else _layers
    nc, used, dbgs = _build(layers)
    f32 = np.float32
    x = np.ascontiguousarray(x, dtype=f32)
    vecs, fw, gmat = _host_inputs(norm_mixer, norm_ffn, pool_scale, kv_norm, rel_bias)
    import ml_dtypes
    kind = np.zeros((128, T), f32)
    for n in range(8):
        kind[n, n * 256:(n + 1) * 256] = 1.0
        kind[64 + n, n * 256:(n + 1) * 256] = 1.0
    shared = {
        "vecs": vecs, "ident_in": np.eye(128, dtype=f32), "fw": fw, "kind": kind.astype(ml_dtypes.bfloat16),
        "final_norm": np.ascontiguousarray(final_norm, dtype=f32).reshape(1, D),
        "pool_w": np.ascontiguousarray(pool_w, dtype=f32), "w_kv": np.ascontiguousarray(w_kv, dtype=f32),
        "w_q": np.ascontiguousarray(w_q, dtype=f32), "w_o": np.ascontiguousarray(w_o, dtype=f32),
        "rel_bias": np.ascontiguousarray(rel_bias, dtype=f32), "gmat": gmat,
        "w_gate_up": np.ascontiguousarray(w_gate_up, dtype=f32),
        "w_down": np.ascontiguousarray(w_down, dtype=f32),
    }
    in_maps = []
    shared = {k: v for k, v in shared.items() if k in used}
    for b in range(NB):
        m = dict(shared)
        m["x"] = x[b]
        in_maps.append(m)
    res = run_bass_kernel_spmd(nc, in_maps, core_ids=list(range(NB)))
    for k in dbgs:
        LAST[k] = [np.asarray(res.results[b][k]) for b in range(NB)]
    out = np.stack([np.asarray(res.results[b]["out"]) for b in range(NB)], axis=0)
    return out.astype(f32)
```

```python
import numpy as np
import concourse.bass as bass
import concourse.mybir as mybir
from concourse.bass_utils import run_bass_kernel_spmd

F32 = mybir.dt.float32
BF16 = mybir.dt.bfloat16
AF = mybir.ActivationFunctionType
ALU = mybir.AluOpType
AX = mybir.AxisListType

D = 1024
T = 2048
NB = 8
DFF = 2816
NCH = 8
EPS = 1e-6
N_A = 2
DEPTH = 4
HD = 64
NH = 16

COMPUTE = ("pe", "act", "dve", "pool")
DMAQ_SLOTS = 8


class Op:
    __slots__ = ("eng", "fn", "dma", "deps", "signal", "sigval", "dslot", "dval", "idx", "qi")

    def __init__(self, eng, fn, dma, idx):
        self.eng = eng
        self.fn = fn
        self.dma = dma
        self.deps = []
        self.signal = False
        self.sigval = 0
        self.dslot = 0
        self.dval = 0
        self.idx = idx
        self.qi = 0


class Prog:
    def __init__(self, nc, strict=True):
        self.nc = nc
        self.ops = []
        self.last_w = {}
        self.readers = {}
        self.strict = strict
        self.qcount = {}

    def add(self, eng, fn, reads=(), writes=(), dma=False):
        op = Op(eng, fn, dma, len(self.ops))
        deps = {}

        def need(o):
            if o is op:
                return
            key = o if o.dma else o.eng
            cur = deps.get(key)
            if cur is None or cur.idx < o.idx:
                deps[key] = o

        for t in reads:
            w = self.last_w.get(t)
            if w is not None:
                need(w)
        for t in writes:
            w = self.last_w.get(t)
            if w is not None:
                need(w)
            for r in self.readers.get(t, {}).values():
                need(r)
        for t in reads:
            self.readers.setdefault(t, {})[op if dma else eng] = op
        for t in writes:
            self.last_w[t] = op
            self.readers[t] = {}
        if dma:
            qi = self.qcount.get(eng, 0)
            self.qcount[eng] = qi + 1
            op.qi = qi
            op.dslot = qi % DMAQ_SLOTS
            op.dval = 16 * (qi // DMAQ_SLOTS + 1)
        op.deps = list(deps.values())
        self.ops.append(op)
        return op

    def emit(self):
        nc = self.nc
        ops = self.ops
        prev_slot = {}
        for op in ops:
            if op.dma:
                k = (op.eng, op.dslot)
                p = prev_slot.get(k)
                if p is not None:
                    op.deps.append(p)
                prev_slot[k] = op
        for op in ops:
            for d in op.deps:
                if not d.dma and (d.eng != op.eng or op.dma or (self.strict and op.eng != "pe")):
                    d.signal = True
        cnt = {}
        for op in ops:
            if not op.dma and op.signal:
                cnt[op.eng] = cnt.get(op.eng, 0) + 1
                op.sigval = cnt[op.eng]
        by_eng = {}
        for op in ops:
            by_eng.setdefault(op.eng, []).append(op)

        import contextlib
        with contextlib.ExitStack() as st:
            esem = {e: st.enter_context(nc.semaphore("s_" + e)) for e in COMPUTE}
            dsem = {}
            for q in self.qcount:
                for s in range(DMAQ_SLOTS):
                    dsem[(q, s)] = st.enter_context(nc.semaphore("d_%s_%d" % (q, s)))
            block = st.enter_context(nc.Block())

            def run(ename, engobj):
                waited = {}
                for op in by_eng.get(ename, []):
                    for d in op.deps:
                        if d.dma:
                            sem, val = dsem[(d.eng, d.dslot)], d.dval
                        else:
                            if d.eng == ename and not (op.dma or (self.strict and ename != "pe")):
                                continue
                            sem, val = esem[d.eng], d.sigval
                        key = id(sem)
                        if waited.get(key, 0) >= val:
                            continue
                        waited[key] = val
                        engobj.wait_ge(sem, val)
                    ins = op.fn(engobj)
                    if op.dma:
                        ins.then_inc(dsem[(ename, op.dslot)], 16)
                    elif op.signal:
                        ins.then_inc(esem[ename], 1)
                last = {}
                for op in by_eng.get(ename, []):
                    if op.dma:
                        last[op.dslot] = op.dval
                for s, v in last.items():
                    engobj.wait_ge(dsem[(ename, s)], v)

            @block.tensor
            def _(e):
                run("pe", e)

            @block.scalar
            def _(e):
                run("act", e)

            @block.vector
            def _(e):
                run("dve", e)

            @block.gpsimd
            def _(e):
                run("pool", e)

            @block.sync
            def _(e):
                run("sp", e)


class Alloc:
    def __init__(self, nc, base=16512, limit=229344):
        self.nc = nc
        self.off = base
        self.limit = limit
        self.n = 0

    def t(self, name, shape, dtype):
        sz = 1
        for s in shape[1:]:
            sz *= s
        sz *= 2 if dtype == BF16 else 4
        off = (self.off + 63) // 64 * 64
        self.n += 1
        h = self.nc.alloc_sbuf_tensor_at("%s_%d" % (name, self.n), list(shape), dtype, offset=off)
        self.off = off + sz
        assert self.off <= self.limit, (name, self.off, self.limit)
        return h

    def fork(self):
        a = Alloc(self.nc, self.off, self.limit)
        a.n = self.n + 1000
        return a


NV = 88
DEBUG = False
ATT_STAGE = 9
ATT_HP = 8
ATT_OWN = 8
HACK = 0
FEED_ATT = False
FEED_FFN_ATT = True
USED = {}
LAST = {}


def NM(l):
    return l * 8


def NF(l):
    return 32 + l * 8


def PSC(l):
    return 64 + l * 8


KVO = 80


def tl(j, n=512):
    return slice(j * n, (j + 1) * n)


class Builder:
    def __init__(self, nc, layers, final=True):
        self.nc = nc
        self.P = Prog(nc)
        self.layers = layers
        self.final = final

    def pe(self, fn, r, w):
        return self.P.add("pe", fn, list(r) + ["ARENA"], w)

    def act(self, fn, r, w):
        return self.P.add("act", fn, list(r) + ["ARENA"], w)

    def dve(self, fn, r, w):
        return self.P.add("dve", fn, list(r) + ["ARENA"], w)

    def pool(self, fn, r, w):
        return self.P.add("pool", fn, list(r) + ["ARENA"], w)

    def dma(self, q, out, in_, r, w):
        return self.P.add(q, lambda e: e.dma_start(out=out, in_=in_), list(r) + ["ARENA"], w, dma=True)

    def barrier(self):
        self.P.add("pool", lambda e: e.memset(self.junk[:], 0.0), [], ["ARENA"])

    def dbg(self, name, ap, shape, reads, dt=F32):
        if not DEBUG:
            return
        d = self.nc.dram_tensor("dbg_" + name, list(shape), dt, kind="ExternalOutput").ap()
        self.dbgs["dbg_" + name] = d
        self.dma("sp", d, ap, reads, [("dbg", name)])

    def mm(self, out, lhsT, rhs, start, stop, r, w):
        return self.pe(lambda e: e.matmul(out, lhsT, rhs, start=start, stop=stop), r, w)

    def wload(self, src, stg, stok, wb, btok):
        self.dma("sp", stg, src, [], [stok])
        self.act(lambda e: e.activation(out=wb, in_=stg, func=AF.Copy), [stok], [btok])

    def build(self):
        nc = self.nc
        specs = {"x": [T, D], "vecs": [128, NV], "ident_in": [128, 128], "fw": [128, 64], "final_norm": [1, D],
                 "pool_w": [2, 4, 256, 256], "w_kv": [D, 2 * D], "w_q": [2, D, D], "w_o": [2, D, D],
                 "rel_bias": [32, 16], "gmat": [128, 16, 256], "w_gate_up": [DEPTH, D, 2 * DFF],
                 "w_down": [DEPTH, DFF, D], "kind": [128, T]}

        class Lazy(dict):
            def __missing__(d, name):
                d[name] = nc.dram_tensor(name, list(specs[name]), BF16 if name == "kind" else F32,
                                         kind="ExternalInput").ap()
                return d[name]

        dr = Lazy()
        dr["out"] = nc.dram_tensor("out", [T, D], F32, kind="ExternalOutput").ap()
        dr["kt_d"] = nc.dram_tensor("kt_d", [8, 128, T], BF16).ap()
        dr["v_d"] = nc.dram_tensor("v_d", [8, 128, 16 * 132], BF16).ap()
        self.dr = dr
        self.dbgs = {}

        A = Alloc(nc)
        self.XT = A.t("XT", [128, NCH, T], F32)
        self.ident = A.t("ident", [128, 128], F32)
        self.identb = A.t("identb", [128, 128], BF16)
        self.vecs = A.t("vecs", [128, NV], F32)
        self.epsc = A.t("epsc", [128, 1], F32)
        self.onesb = A.t("onesb", [128, 128], BF16)
        self.RS = A.t("RS", [128, T], F32)
        self.KMBD = A.t("KMBD", [128, 2, 8, 16], BF16)
        self.nb31 = A.t("nb31", [128, 16], F32)
        self.FW = A.t("FW", [128, 64], F32)
        self.junk = A.t("junk", [128, 1], F32)
        self.sqb = A.t("sqb", [128, NCH, 512], BF16)
        self.ps = [nc.alloc_psum_tensor("ps%d" % i, [128, 512], F32) for i in range(8)]

        self.dma("sp", self.ident[:], dr["ident_in"], [], ["ident"])
        self.dma("sp", self.vecs[:], dr["vecs"], [], ["vecs"])
        self.dma("sp", self.FW[:], dr["fw"], [], ["FW"])
        self.dma("sp", self.nb31[:], dr["rel_bias"][31:32, :].partition_broadcast(128).rearrange("p o d -> p (o d)"),
                 [], ["nb31"])
        self.dve(lambda e: e.tensor_scalar(out=self.nb31[:], in0=self.nb31[:], scalar1=-1.0, scalar2=None,
                                           op0=ALU.mult), ["nb31"], ["nb31"])
        self.pool(lambda e: e.memset(self.epsc[:], EPS), [], ["epsc"])
        self.pool(lambda e: e.memset(self.onesb[:], 1.0 / D), [], ["onesb"])
        self.act(lambda e: e.activation(out=self.identb[:], in_=self.ident[:], func=AF.Copy), ["ident"], ["identb"])

        self.load_x(A.fork())
        self.rs_valid = False

        def need_rs():
            if not self.rs_valid:
                self.rs_tiles()
            self.rs_valid = False

        nl = len(self.layers)
        for li, ll in enumerate(self.layers):
            l, part = (ll, "mf") if isinstance(ll, int) else ll
            if "m" in part:
                need_rs()
                if l < N_A:
                    self.pool_mixer(l, A.fork(), feed="f" in part)
                else:
                    if l == N_A:
                        self.kv_phase(A.fork())
                    self.attention(l, A.fork(), feed=("f" in part) and FEED_ATT)
                self.rs_valid = ("f" in part) and (FEED_ATT or l < N_A)
            if "k" in part:
                need_rs()
                self.kv_phase(A.fork())
                self.rs_valid = True
            if "a" in part:
                need_rs()
                self.attention(l, A.fork())
            if "f" in part:
                need_rs()
                fd = li + 1 < nl
                if fd and not FEED_FFN_ATT:
                    nxt = self.layers[li + 1]
                    nxl = nxt if isinstance(nxt, int) else nxt[0]
                    fd = nxl < N_A
                self.ffn(l, A.fork(), feed=fd)
                self.rs_valid = fd
        self.final_out(A.fork())
        self.P.emit()

    def load_x(self, A):
        self.barrier()
        xin = [A.t("xin", [128, D], F32) for _ in range(3)]
        xv = self.dr["x"].rearrange("(n p) d -> n p d", p=128)
        for n in range(T // 128):
            b = xin[n % 3]
            tk = ("xin", n % 3)
            self.dma("sp", b[:], xv[n], [], [tk])
            for half in range(2):
                pi = (n * 2 + half) % 4
                pt = self.ps[pi]
                ptk = ("ps", pi)
                for cc in range(4):
                    c = half * 4 + cc
                    self.pe(lambda e, pt=pt, b=b, c=c, cc=cc: e.transpose(
                        pt[:, cc * 128:(cc + 1) * 128], b[:, c * 128:(c + 1) * 128], self.ident[:]),
                        [tk, "ident"], [ptk])
                outap = self.XT[:, half * 4:(half + 1) * 4, n * 128:(n + 1) * 128]
                inap = pt[:].rearrange("p (c t) -> p c t", c=4)
                wt = [("XT", half * 4 + cc, n // 4) for cc in range(4)]
                if (n + half) % 2 == 0:
                    self.dve(lambda e, o=outap, i=inap: e.tensor_copy(out=o, in_=i), [ptk], wt)
                else:
                    self.act(lambda e, o=outap, i=inap: e.activation(out=o, in_=i, func=AF.Copy), [ptk], wt)

    def final_out(self, A):
        self.barrier()
        gfin = A.t("gfin", [128, D], F32)
        self.dma("sp", gfin[:], self.dr["final_norm"].partition_broadcast(128).rearrange("p o d -> p (o d)"),
                 [], ["gfin"])
        ob = [A.t("ob", [128, D], F32) for _ in range(2)]
        sq = A.t("sqjunk", [128, 512], F32)
        ssum = [A.t("ssum", [128, 2], F32) for _ in range(2)]
        ov = self.dr["out"].rearrange("(n p) d -> n p d", p=128)
        for n in range(T // 128):
            pts = [self.ps[(n % 2) * 2 + i] for i in range(2)]
            ptk = [("ps", (n % 2) * 2 + i) for i in range(2)]
            for c in range(NCH):
                half = c // 4
                self.pe(lambda e, pt=pts[half], c=c, n=n: e.transpose(
                    pt[:, (c % 4) * 128:(c % 4 + 1) * 128], self.XT[:, c, n * 128:(n + 1) * 128], self.ident[:]),
                    [("XT", c, n // 4), "ident"], [ptk[half]])
            s = ssum[n % 2]
            stk = ("ssum", n % 2)
            o = ob[n % 2]
            otk = ("ob", n % 2)
            for half in range(2):
                self.act(lambda e, pt=pts[half], s=s, half=half: e.activation(
                    out=sq[:], in_=pt[:], func=AF.Square, accum_out=s[:, half:half + 1]),
                    [ptk[half]], [stk, "sqjunk"])
            self.dve(lambda e, s=s: e.tensor_tensor(out=s[:, 0:1], in0=s[:, 0:1], in1=s[:, 1:2], op=ALU.add),
                     [stk], [stk])
            self.act(lambda e, s=s: e.activation(out=s[:, 0:1], in_=s[:, 0:1], func=AF.Sqrt,
                                                 bias=self.epsc[:], scale=1.0 / D), [stk, "epsc"], [stk])
            self.dve(lambda e, s=s: e.reciprocal(out=s[:, 0:1], in_=s[:, 0:1]), [stk], [stk])
            for half in range(2):
                self.dve(lambda e, pt=pts[half], s=s, o=o, half=half: e.scalar_tensor_tensor(
                    out=o[:, half * 512:(half + 1) * 512], in0=pt[:], scalar=s[:, 0:1],
                    in1=gfin[:, half * 512:(half + 1) * 512], op0=ALU.mult, op1=ALU.mult),
                    [ptk[half], stk, "gfin"], [otk])
            self.dma("sp", ov[n], o[:], [otk], [("out", n)])

    def rs_tiles(self, A=None):
        for j in range(4):
            b = self.sqb
            for c in range(NCH):
                xa = self.XT[:, c, tl(j)]
                if c in (0, 3, 6):
                    self.act(lambda e, o=b[:, c, :], xa=xa: e.activation(out=o, in_=xa, func=AF.Square),
                             [("XT", c, j)], [("sqb", c)])
                else:
                    fn = lambda e, o=b[:, c, :], xa=xa: e.tensor_tensor(out=o, in0=xa, in1=xa, op=ALU.mult)
                    (self.pool if c in (1, 5) else self.dve)(fn, [("XT", c, j)], [("sqb", c)])
            pb = 6 + (j % 2)
            for c in range(NCH):
                self.mm(self.ps[pb][:], self.onesb[:], b[:, c, :], c == 0, c == NCH - 1,
                        [("sqb", c), "onesb"], [("ps", pb)])
            self.act(lambda e, pb=pb, j=j: e.activation(out=self.RS[:, tl(j)], in_=self.ps[pb][:], func=AF.Ln,
                                                        bias=self.epsc[:], scale=1.0),
                     [("ps", pb), "epsc"], [("RS", j)])
            self.act(lambda e, j=j: e.activation(out=self.RS[:, tl(j)], in_=self.RS[:, tl(j)], func=AF.Exp,
                                                 scale=-0.5), [("RS", j)], [("RS", j)])

    def rs_feeder(self, banks):
        B = self
        st = {"k": 0, "pend": []}

        def square(c, j):
            slot = st["k"] % NCH
            st["k"] += 1
            o = B.sqb[:, slot, :]
            xa = B.XT[:, c, tl(j)]
            B.act(lambda e, o=o, xa=xa: e.activation(out=o, in_=xa, func=AF.Square), [("XT", c, j)], [("sqb", slot)])
            st["pend"].append((c, j, slot))

        def flush(keep=0):
            while len(st["pend"]) > keep:
                c, j, slot = st["pend"].pop(0)
                bank = banks[j]
                B.mm(B.ps[bank][:], B.onesb[:], B.sqb[:, slot, :], c == 0, c == NCH - 1,
                     [("sqb", slot), "onesb"], [("ps", bank)])
                if c == NCH - 1:
                    B.act(lambda e, bank=bank, j=j: e.activation(out=B.RS[:, tl(j)], in_=B.ps[bank][:], func=AF.Ln,
                                                                 bias=B.epsc[:], scale=1.0),
                          [("ps", bank), "epsc"], [("RS", j)])
                    B.act(lambda e, j=j: e.activation(out=B.RS[:, tl(j)], in_=B.RS[:, tl(j)], func=AF.Exp,
                                                      scale=-0.5), [("RS", j)], [("RS", j)])
        return square, flush

    def make_hT(self, HT, name, goff, j, jj):
        for c in range(NCH):
            self.dve(lambda e, c=c: e.scalar_tensor_tensor(
                out=HT[:, c, tl(jj)], in0=self.XT[:, c, tl(j)], scalar=self.vecs[:, goff + c:goff + c + 1],
                in1=self.RS[:, tl(j)], op0=ALU.mult, op1=ALU.mult),
                [("XT", c, j), ("RS", j), "vecs"], [(name, c, jj)])

    def ffn(self, l, A, feed=False):
        self.barrier()
        fsq, fflush = self.rs_feeder({0: 6, 1: 7, 2: 6, 3: 7})
        dr = self.dr
        HT = A.t("HTF", [128, NCH, 1024], BF16)
        ACTT = A.t("ACTT", [128, 22, 1024], BF16)
        stg = [A.t("gstg", [128, NCH, 256], F32) for _ in range(2)]
        wb = [A.t("gwb", [128, NCH, 256], BF16) for _ in range(4)]
        dstg = A.t("dstg", [128, 22, 128], F32)
        dwb = [A.t("dwb", [128, 22, 128], BF16) for _ in range(2)]
        sig = [A.t("sig", [128, 512], F32) for _ in range(2)]
        wgu = dr["w_gate_up"][l].rearrange("(c p) f -> p c f", p=128)
        wd = dr["w_down"][l].rearrange("(k p) d -> p k d", p=128)
        units = []
        for g in range(11):
            units += [(g, 0), (g, 1)]

        def load_unit(i):
            g, gu = units[i]
            col = gu * DFF + g * 256
            self.wload(wgu[:, :, col:col + 256], stg[i % 2][:], ("gstg", i % 2), wb[i % 4][:], ("gwb", i % 4))

        def load_d(oc):
            self.wload(wd[:, :, oc * 128:(oc + 1) * 128], dstg[:], "dstg", dwb[oc % 2][:], ("dwb", oc % 2))

        load_unit(0)
        load_unit(1)
        for jj in range(2):
            self.make_hT(HT, "HTF", NF(l), jj, jj)
        for s in range(2):
            cnt = 0
            for g in range(11):
                if g == 8:
                    load_d(0)
                for k in (2 * g + 2, 2 * g + 3):
                    if k < 22:
                        load_unit(k)
                ig, iu = 2 * g, 2 * g + 1
                for fc in range(2):
                    f = 2 * g + fc
                    for tt in range(2):
                        r = cnt % 2
                        cnt += 1
                        pg, pu = r, 2 + r
                        for (pbk, iw) in ((pg, ig), (pu, iu)):
                            for c in range(NCH):
                                self.mm(self.ps[pbk][:], wb[iw % 4][:, c, fc * 128:(fc + 1) * 128], HT[:, c, tl(tt)],
                                        c == 0, c == NCH - 1, [("gwb", iw % 4), ("HTF", c, tt)], [("ps", pbk)])
                        self.act(lambda e, r=r, pg=pg: e.activation(out=sig[r][:], in_=self.ps[pg][:], func=AF.Silu),
                                 [("ps", pg)], [("sig", r)])
                        self.dve(lambda e, r=r, pu=pu, f=f, tt=tt: e.tensor_tensor(
                            out=ACTT[:, f, tl(tt)], in0=sig[r][:], in1=self.ps[pu][:], op=ALU.mult),
                            [("sig", r), ("ps", pu)], [("ACTT", f, tt)])

            if s == 0:
                load_unit(0)
                load_unit(1)
                for jj in range(2):
                    self.make_hT(HT, "HTF", NF(l), 2 + jj, jj)
            for oc in range(8):
                if oc + 1 < 8:
                    load_d(oc + 1)
                for tt in range(2):
                    pb = 4 + (oc * 2 + tt) % 2
                    j = 2 * s + tt
                    for k in range(22):
                        self.mm(self.ps[pb][:], dwb[oc % 2][:, k, :], ACTT[:, k, tl(tt)], k == 0, k == 21,
                                [("dwb", oc % 2), ("ACTT", k, tt)], [("ps", pb)])
                    if feed:
                        fflush(0)
                    self.dve(lambda e, pb=pb, oc=oc, j=j: e.tensor_tensor(
                        out=self.XT[:, oc, tl(j)], in0=self.ps[pb][:], in1=self.XT[:, oc, tl(j)], op=ALU.add),
                        [("ps", pb), ("XT", oc, j)], [("XT", oc, j)])
                    if feed:
                        fsq(oc, j)
            if feed:
                fflush(0)

    def pool_mixer(self, l, A, feed=False):
        self.barrier()
        fsq, fflush = self.rs_feeder({0: 4, 1: 5, 2: 6, 3: 7})
        dr = self.dr
        HCs = [A.t("HC", [128, 16 + T], F32) for _ in range(2)]
        Ss = [[A.t("SW", [128, 16 + T], F32) for _ in range(2)] for _ in range(2)]
        DT = [A.t("DT", [128, 2, T], BF16) for _ in range(2)]
        pstg = A.t("pstg", [128, 8, 256], F32)
        PW = A.t("PW", [128, 8, 256], BF16)
        tmp16s = [A.t("tmp16", [128, 16], F32) for _ in range(2)]
        self.wload(dr["pool_w"][l].rearrange("g (ci p) e -> p (g ci) e", p=128), pstg[:], "pstg", PW[:], "PW")
        for q in range(2):
            self.pool(lambda e, q=q: e.memset(HCs[q][:, 0:16], 0.0), [], [("HC", q)])
            for i in range(2):
                self.pool(lambda e, q=q, i=i: e.memset(Ss[q][i][:, 0:16], 0.0), [], [("SW", q, i)])
        allrs = [("RS", j) for j in range(4)]

        def emit_HC(g, ci):
            c = 2 * g + ci
            HC, hctk = HCs[ci], ("HC", ci)
            self.dve(lambda e, c=c, HC=HC: e.scalar_tensor_tensor(
                out=HC[:, 16:], in0=self.XT[:, c, :], scalar=self.vecs[:, NM(l) + c:NM(l) + c + 1],
                in1=self.RS[:, :], op0=ALU.mult, op1=ALU.mult),
                [("XT", c, j) for j in range(4)] + allrs + ["vecs"], [hctk])

        def emit_adds(g, ci):
            w = 2 << g
            S = Ss[ci]
            src, stok = HCs[ci], ("HC", ci)
            sh = 1
            k = 0
            while sh < w:
                dst, dtok = S[k % 2], ("SW", ci, k % 2)
                fn = lambda e, dst=dst, src=src, sh=sh: e.tensor_tensor(
                    out=dst[:, 16:], in0=src[:, 16:], in1=src[:, 16 - sh:16 - sh + T], op=ALU.add)
                (self.dve if (g == 3 and sh * 2 >= w) else self.pool)(fn, [stok], [dtok])
                src, stok = dst, dtok
                sh *= 2
                k += 1
            return src, stok

        def emit_D(g, ci, src, stok):
            w = 2 << g
            dt = DT[g % 2]
            HC, hctk = HCs[ci], ("HC", ci)
            tmp16, t16tk = tmp16s[ci], ("tmp16", ci)
            self.dve(lambda e, src=src, ci=ci, dt=dt, w=w, HC=HC: e.scalar_tensor_tensor(
                out=dt[:, ci, :], in0=src[:, 16:], scalar=1.0 / w, in1=HC[:, 16:], op0=ALU.mult,
                op1=ALU.subtract), [stok, hctk], [("DT", g % 2, ci)])
            self.dve(lambda e, src=src, g=g, tmp16=tmp16: e.tensor_tensor(
                out=tmp16[:], in0=src[:, 16:32], in1=self.FW[:, g * 16:(g + 1) * 16], op=ALU.mult),
                [stok, "FW"], [t16tk])
            self.dve(lambda e, ci=ci, dt=dt, tmp16=tmp16, HC=HC: e.tensor_tensor(
                out=dt[:, ci, 0:16], in0=tmp16[:], in1=HC[:, 16:32], op=ALU.subtract),
                [t16tk, hctk], [("DT", g % 2, ci)])

        def emit_mm(g):
            dt = DT[g % 2]
            for eo in range(2):
                co = 2 * g + eo
                for j in range(4):
                    pb = (eo * 4 + j) % 4
                    for ci in range(2):
                        self.mm(self.ps[pb][:], PW[:, 2 * g + ci, eo * 128:(eo + 1) * 128], dt[:, ci, tl(j)],
                                ci == 0, ci == 1, ["PW", ("DT", g % 2, ci)], [("ps", pb)])
                    if feed:
                        fflush(2)
                    self.dve(lambda e, pb=pb, co=co, j=j: e.scalar_tensor_tensor(
                        out=self.XT[:, co, tl(j)], in0=self.ps[pb][:],
                        scalar=self.vecs[:, PSC(l) + co:PSC(l) + co + 1], in1=self.XT[:, co, tl(j)],
                        op0=ALU.mult, op1=ALU.add), [("ps", pb), ("XT", co, j), "vecs"], [("XT", co, j)])
                    if feed:
                        fsq(co, j)

        emit_HC(0, 0)
        emit_HC(0, 1)
        for g in range(4):
            r0 = emit_adds(g, 0)
            r1 = emit_adds(g, 1)
            emit_D(g, 0, *r0)
            if g < 3:
                emit_HC(g + 1, 0)
            emit_D(g, 1, *r1)
            if g < 3:
                emit_HC(g + 1, 1)
            emit_mm(g)
        if feed:
            fflush(0)

    def kv_phase(self, A):
        self.barrier()
        dr = self.dr
        HT = A.t("HTK", [128, NCH, T], BF16)
        kstg = [A.t("kstg", [128, NCH, 128], F32) for _ in range(2)]
        kwb = [A.t("kwb", [128, NCH, 128], BF16) for _ in range(3)]
        KTB = [A.t("KTB", [128, T], BF16) for _ in range(2)]
        VTS = [A.t("VTS", [128, T], BF16) for _ in range(2)]
        VB = [A.t("VB", [128, 16, 132], BF16) for _ in range(2)]
        kmf = A.t("kmf", [128, 8], F32)
        wkv = dr["w_kv"].rearrange("(c p) f -> p c f", p=128)
        units = []
        for hp in range(ATT_HP):
            units += [(hp, 0), (hp, 1)]

        def load(i):
            hp, kv = units[i]
            col = kv * D + hp * 128
            self.wload(wkv[:, :, col:col + 128], kstg[i % 2][:], ("kstg", i % 2), kwb[i % 3][:], ("kwb", i % 3))

        load(0)
        load(1)
        for jx in range(4):
            self.make_hT(HT, "HTK", KVO, jx, jx)
        for i in range(2):
            self.pool(lambda e, i=i: e.memset(VB[i][:], 1.0), [], [("VB", i)])
        self.pool(lambda e: e.memset(self.KMBD[:], 0.0), [], ["KMBD"])
        nb = 0
        for i, (hp, kv) in enumerate(units):
            if i + 2 < len(units):
                load(i + 2)
            w = kwb[i % 3]
            wtk = ("kwb", i % 3)
            if kv == 0:
                ktb = KTB[hp % 2]
                for jx in range(4):
                    pb = nb % 4
                    nb += 1
                    for c in range(NCH):
                        self.mm(self.ps[pb][:], w[:, c, :], HT[:, c, tl(jx)], c == 0, c == NCH - 1,
                                [wtk, ("HTK", c, jx)], [("ps", pb)])
                    self.act(lambda e, pb=pb, jx=jx, ktb=ktb: e.activation(out=ktb[:, tl(jx)], in_=self.ps[pb][:],
                                                                            func=AF.Copy),
                             [("ps", pb)], [("KTB", hp % 2)])
                self.dma("sp", dr["kt_d"][hp], ktb[:], [("KTB", hp % 2)], [("kt_d", hp)])
                self.dve(lambda e, ktb=ktb: e.tensor_reduce(
                    out=kmf[:, 0:8], in_=ktb[:].rearrange("p (n k) -> p n k", k=256), axis=AX.X, op=ALU.add),
                    [("KTB", hp % 2)], ["kmf"])
                self.dve(lambda e, hp=hp: e.tensor_scalar(out=self.KMBD[0:64, 0, hp, 0:8], in0=kmf[0:64, :],
                                                          scalar1=1.0 / 256, scalar2=None, op0=ALU.mult),
                         ["kmf"], ["KMBD"])
                self.dve(lambda e, hp=hp: e.tensor_scalar(out=self.KMBD[64:128, 1, hp, 8:16], in0=kmf[64:128, :],
                                                          scalar1=1.0 / 256, scalar2=None, op0=ALU.mult),
                         ["kmf"], ["KMBD"])
            else:
                vts = VTS[hp % 2]
                vb = VB[hp % 2]
                for jx in range(4):
                    pb = nb % 4
                    nb += 1
                    for c in range(NCH):
                        self.mm(self.ps[pb][:], w[:, c, :], HT[:, c, tl(jx)], c == 0, c == NCH - 1,
                                [wtk, ("HTK", c, jx)], [("ps", pb)])
                    self.dve(lambda e, pb=pb, jx=jx, vts=vts: e.tensor_copy(out=vts[:, tl(jx)], in_=self.ps[pb][:]),
                             [("ps", pb)], [("VTS", hp % 2, jx)])
                for half in range(2):
                    bank = 4 + half
                    psb = self.ps[bank][:].bitcast(BF16)
                    for t8 in range(8):
                        tk = half * 8 + t8
                        self.pe(lambda e, psb=psb, t8=t8, tk=tk, vts=vts: e.transpose(
                            psb[:, t8 * 128:(t8 + 1) * 128], vts[:, tk * 128:(tk + 1) * 128], self.identb[:]),
                            [("VTS", hp % 2, tk // 4), "identb"], [("ps", bank)])
                    oap = vb[:, half * 8:(half + 1) * 8, :].rearrange("p t (h e) -> p t h e", e=66)[:, :, :, 0:64]
                    iap = psb[:, 0:1024].rearrange("p (t h e) -> p t h e", t=8, h=2)
                    if half == 0:
                        self.act(lambda e, oap=oap, iap=iap: e.activation(out=oap, in_=iap, func=AF.Copy),
                                 [("ps", bank)], [("VB", hp % 2)])
                    else:
                        self.dve(lambda e, oap=oap, iap=iap: e.tensor_copy(out=oap, in_=iap),
                                 [("ps", bank)], [("VB", hp % 2)])
                for hf in range(2):
                    self.dma("sp", dr["v_d"][hp][:, hf * 1056:(hf + 1) * 1056],
                             vb[:, hf * 8:(hf + 1) * 8, :].rearrange("p t e -> p (t e)"), [("VB", hp % 2)],
                             [("v_d", hp, hf)])

    def attention(self, l, A, feed=False):
        self.barrier()
        fsq, fflush = self.rs_feeder({0: 0, 1: 1, 2: 2, 3: 3})
        dr = self.dr
        jl = l - N_A
        PTb = [A.t("PT", [128, 8, 2, 256], BF16) for _ in range(2)]
        PF = A.t("PF", [128, 512], F32)
        PFS = A.t("PFS", [128, 128], F32)
        ON = [A.t("ON", [128, 2, 128], BF16) for _ in range(2)]
        HT = A.t("HTQ", [128, NCH, T], BF16)
        OT = A.t("OT", [128, NCH, T], BF16)
        KTv = [A.t("KTA", [128, T], BF16), A.t("KTB", [128, T], BF16)]
        QTa = [A.t("QT", [128, T], BF16) for _ in range(2)]
        QTb = [[QTa[0][:], QTa[1][:]],
               [self.sqb[:, 0:4, :].rearrange("p c t -> p (c t)"), self.sqb[:, 4:8, :].rearrange("p c t -> p (c t)")]]
        V = [A.t("V", [128, 16, 132], BF16) for _ in range(2)]
        qstg = A.t("qstg", [128, NCH, 64], F32)
        GST = A.t("GST", [128, 2, 256], F32)
        qwb = [A.t("qwb", [128, NCH, 128], BF16) for _ in range(2)]
        EG = A.t("EG", [128, 2, 2, 256], F32)
        RINV = [A.t("RINV", [128, 2], F32) for _ in range(2)]
        GT = A.t("GT", [128, 16, 8], F32)
        GT2 = A.t("GT2", [128, 16, 8], F32)
        MK = A.t("MK", [128, 16, 8], F32)
        MX = A.t("MX", [128, 16], F32)
        SEL = A.t("SEL", [128, 16, 8], F32)
        INV = A.t("INV", [128, 16, 8], F32)
        M128 = A.t("M128", [128, 4, 2, 128], BF16)
        for j in range(4):
            self.make_hT(HT, "HTQ", NM(l), j, j)
        self.pool(lambda e: e.memset(EG[:, :, 1, 0:128], 0.0), [], ["EG"])
        for b in range(2):
            self.pool(lambda e, b=b: e.memset(QTb[b][0][64:128, :], 0.0), [("sqb", c) for c in range(NCH)],
                      [("QT", b, o) for o in range(8)] + [("sqb", c) for c in range(NCH)])
            self.pool(lambda e, b=b: e.memset(QTb[b][1][0:64, :], 0.0), [], [("QT", b, o) for o in range(8)])
        self.pool(lambda e: e.memset(M128[:], 0.0), [], ["M128"])
        self.pool(lambda e: e.memset(INV[:], 0.0), [], ["INV"])
        for o in range(4, 8):
            self.pool(lambda e, o=o: e.memset(INV[:, (o - 4) * 4:(o - 4) * 4 + 4, o:8], 1.0), [], ["INV"])
        self.dma("sp", KTv[0][64:128, :], dr["kind"][64:128, :], [], ["KTA"])
        self.dma("sp", KTv[1][0:64, :], dr["kind"][0:64, :], [], ["KTB"])
        wq = dr["w_q"][jl].rearrange("(c p) f -> p c f", p=128)
        wo = dr["w_o"][jl].rearrange("(c p) f -> p c f", p=128)
        ps7b = self.ps[7][:].bitcast(BF16)
        ps6b = self.ps[6][:].bitcast(BF16)
        cnt = {"sb": 0, "slot": 0, "pf": 0, "unit": 0}
        for hp in range(ATT_HP):
            v = V[hp % 2]
            vtk = ("V", hp % 2)
            self.dma("sp", KTv[0][0:64, :], dr["kt_d"][hp][0:64, :], [("kt_d", hp)], ["KTA"])
            self.dma("sp", KTv[1][64:128, :], dr["kt_d"][hp][64:128, :], [("kt_d", hp)], ["KTB"])
            for hf in range(2):
                self.dma("sp", v[:, hf * 8:(hf + 1) * 8, :].rearrange("p t e -> p (t e)"),
                         dr["v_d"][hp][:, hf * 1056:(hf + 1) * 1056], [("v_d", hp, hf)], [vtk])
            if hp == 0:
                self.dma("sp", GST[:], dr["gmat"][:, 0:2, :], [], ["GST"])
            for hh in range(2):
                h = 2 * hp + hh
                self.act(lambda e, hh=hh, h=h: e.activation(out=EG[:, hh, 0, :], in_=GST[:, hh, :], func=AF.Exp,
                                                            bias=self.nb31[:, h:h + 1], scale=1.0),
                         ["GST", "nb31"], ["EG"])
                self.act(lambda e, hh=hh: e.activation(out=EG[:, hh, 1, 128:256], in_=EG[:, hh, 0, 0:128],
                                                       func=AF.Copy), ["EG"], ["EG"])
            if hp + 1 < 8:
                self.dma("sp", GST[:], dr["gmat"][:, 2 * hp + 2:2 * hp + 4, :], [], ["GST"])

            def prologue(hp):
                QT = QTb[hp % 2]
                qb = hp % 2
                chunks = []

                def c0():
                    for hf in range(2):
                        self.wload(wq[:, :, hp * 128 + hf * 64:hp * 128 + (hf + 1) * 64], qstg[:], "qstg",
                                   qwb[hp % 2][:, :, hf * 64:(hf + 1) * 64], ("qwb", hp % 2))
                chunks.append(c0)

                def cj(j):
                    def f():
                        for c in range(NCH):
                            self.mm(self.ps[7][:], qwb[hp % 2][:, c, :], HT[:, c, tl(j)], c == 0, c == NCH - 1,
                                    [("qwb", hp % 2), ("HTQ", c, j)], [("ps", 7)])
                        for hq in range(2):
                            self.dve(lambda e, hq=hq: e.tensor_scalar(
                                out=QT[hq][64 * hq:64 * hq + 64, tl(j)], in0=self.ps[7][64 * hq:64 * hq + 64, :],
                                scalar1=0.125, scalar2=None, op0=ALU.mult), [("ps", 7)],
                                [("QT", qb, 2 * j), ("QT", qb, 2 * j + 1)])
                    return f
                for j in range(4):
                    chunks.append(cj(j))

                def cg():
                    for o in range(4, 8):
                        for qs in range(2):
                            col = (o - 4) * 32 + qs * 16
                            for hq in range(2):
                                self.mm(self.ps[7][:, col:col + 16],
                                        QT[hq][:, o * 256 + qs * 128:o * 256 + (qs + 1) * 128],
                                        self.KMBD[:, hq, hp, :], hq == 0, hq == 1, [("QT", qb, o), "KMBD"],
                                        [("ps", 7)])
                    self.pool(lambda e: e.memset(GT[:], -1e30), [], ["GT"])
                    for o in range(4, 8):
                        self.dve(lambda e, o=o: e.tensor_copy(
                            out=GT[:, (o - 4) * 4:(o - 4) * 4 + 4, 0:o],
                            in_=self.ps[7][:, (o - 4) * 32:(o - 4) * 32 + 32].rearrange(
                                "p (a n) -> p a n", n=8)[:, :, 0:o]), [("ps", 7)], ["GT"])
                    cur = GT
                    for it in range(3):
                        self.dve(lambda e, cur=cur: e.tensor_reduce(out=MX[:], in_=cur[:], axis=AX.X, op=ALU.max),
                                 ["GT"], ["GT"])
                        if it == 2:
                            break
                        self.dve(lambda e, cur=cur: e.tensor_tensor(
                            out=MK[:], in0=cur[:], in1=MX[:].unsqueeze(2).broadcast_to([128, 16, 8]), op=ALU.is_ge),
                            ["GT"], ["GT"])
                        self.dve(lambda e, cur=cur: e.scalar_tensor_tensor(
                            out=GT2[:], in0=MK[:], scalar=-1e30, in1=cur[:], op0=ALU.mult, op1=ALU.add),
                            ["GT"], ["GT"])
                        cur = GT2
                    self.dve(lambda e: e.tensor_tensor(
                        out=SEL[:], in0=GT[:], in1=MX[:].unsqueeze(2).broadcast_to([128, 16, 8]), op=ALU.is_ge),
                        ["GT"], ["SEL"])
                    self.dve(lambda e: e.tensor_tensor(out=SEL[:], in0=SEL[:], in1=INV[:], op=ALU.max),
                             ["SEL", "INV"], ["SEL"])
                    selv = SEL[:].rearrange("p (o q h) n -> p o q h n", o=4, q=2)
                    for hh in range(2):
                        base = 64 if hh == 0 else 0
                        self.dve(lambda e, hh=hh, base=base: e.tensor_scalar(
                            out=M128[:, :, :, base:base + 8], in0=selv[:, :, :, hh, :], scalar1=-1.0,
                            scalar2=30000.0, op0=ALU.add, op1=ALU.mult), ["SEL"], ["M128"])
                chunks.append(cg)
                return chunks

            if hp == 0:
                for ch in prologue(0):
                    ch()
            next_chunks = prologue(hp + 1) if hp + 1 < 8 else []
            QT = QTb[hp % 2]
            qb = hp % 2
            nsel = 4
            def emit_mask_rows():
                for o in range(4, 4 + nsel):
                    for qs in range(2):
                        cb = (o - 4) * 256 + qs * 128
                        self.pe(lambda e, o=o, qs=qs, cb=cb: e.transpose(ps7b[:, cb:cb + 128], M128[:, o - 4, qs, :],
                                                                         self.identb[:]),
                                ["M128", "identb"], [("ps", 7)])
                n = nsel * 256
                oa = QT[0][64:72, 1024:1024 + n]
                ob = QT[1][0:8, 1024:1024 + n]
                ia = ps7b[64:72, 0:n]
                ib = ps7b[0:8, 0:n]
                wtk = [("QT", qb, o) for o in range(4, 4 + nsel)]
                self.act(lambda e, oa=oa, ia=ia: e.activation(out=oa, in_=ia, func=AF.Copy), [("ps", 7)], wtk)
                self.act(lambda e, ob=ob, ib=ib: e.activation(out=ob, in_=ib, func=AF.Copy), [("ps", 7)], wtk)

            own_order = [0, 7, 1, 6, 2, 5, 3, 4]
            units = []
            for a, b in ((0, 7), (1, 6), (2, 5), (3, 4)):
                units += [(a, 0), (b, 0), (a, 1), (b, 1)]
            onidx = {own: k % 2 for k, own in enumerate(own_order)}

            def emit_block_scores(own, hh, n, pbi):
                q0 = own * 256
                PT = PTb[pbi]
                sb = cnt["sb"] % 3
                cnt["sb"] += 1
                for k2 in range(2):
                    kk = n * 256 + k2 * 128
                    self.mm(self.ps[sb][:, k2 * 256:(k2 + 1) * 256], KTv[hh][:, kk:kk + 128],
                            QT[hh][:, q0:q0 + 256], True, True, ["KTA" if hh == 0 else "KTB", ("QT", qb, own)],
                            [("ps", sb)])
                ptk = ("PT", pbi, n)
                if n == own:
                    self.act(lambda e, sb=sb: e.activation(out=PF[:], in_=self.ps[sb][:], func=AF.Exp),
                             [("ps", sb)], ["PF"])
                    self.dve(lambda e, PT=PT, n=n, hh=hh: e.tensor_tensor(
                        out=PT[:, n, :, :], in0=PF[:].rearrange("p (a b) -> p a b", a=2),
                        in1=EG[:, hh, :, :], op=ALU.mult), ["PF", "EG"], [ptk])
                elif n == own - 1:
                    self.act(lambda e, sb=sb, PT=PT, n=n: e.activation(
                        out=PT[:, n, :, :], in_=self.ps[sb][:].rearrange("p (a b) -> p a b", a=2),
                        func=AF.Exp), [("ps", sb)], [ptk])
                    self.act(lambda e, sb=sb: e.activation(out=PFS[:, 0:128], in_=self.ps[sb][:, 256:384],
                                                          func=AF.Exp), [("ps", sb)], ["PFS"])
                    self.dve(lambda e, PT=PT, n=n, hh=hh: e.tensor_tensor(
                        out=PT[:, n, 1, 0:128], in0=PFS[:, 0:128], in1=EG[:, hh, 0, 128:256], op=ALU.mult),
                        ["PFS", "EG"], [ptk])
                else:
                    self.act(lambda e, sb=sb, PT=PT, n=n: e.activation(
                        out=PT[:, n, :, :], in_=self.ps[sb][:].rearrange("p (a b) -> p a b", a=2),
                        func=AF.Exp), [("ps", sb)], [ptk])

            def emit_pv_block(u, n):
                own, hh, pbi, par = u
                PT = PTb[pbi]
                for qs in range(2):
                    bank = 3 + 2 * par + qs
                    for k2 in range(2):
                        if n == own and k2 == 1 and qs == 0:
                            continue
                        bo = border(own)
                        first = (n == bo[0] and k2 == 0)
                        last = (n == bo[-1] and k2 == (1 if (qs == 1 or n != own) else 0))
                        self.mm(self.ps[bank][:, 0:66], PT[:, n, k2, qs * 128:(qs + 1) * 128],
                                v[:, n * 2 + k2, hh * 66:(hh + 1) * 66], first, last,
                                [("PT", pbi, n), vtk], [("ps", bank)])

            def emit_pv_finish(u, hp=hp):
                own, hh, pbi, par = u
                on = ON[onidx[own]]
                ontk = ("ON", onidx[own])
                rinv = RINV[par]
                rtk = ("RINV", par)
                for qs in range(2):
                    bank = 3 + 2 * par + qs
                    self.dve(lambda e, bank=bank, qs=qs, rinv=rinv: e.reciprocal(
                        out=rinv[:, qs:qs + 1], in_=self.ps[bank][:, 64:65]), [("ps", bank)], [rtk])
                    self.dve(lambda e, bank=bank, qs=qs, rinv=rinv, on=on, hh=hh: e.tensor_scalar(
                        out=on[:, qs, hh * 64:(hh + 1) * 64], in0=self.ps[bank][:, 0:64], scalar1=rinv[:, qs:qs + 1],
                        scalar2=None, op0=ALU.mult), [("ps", bank), rtk], [ontk])
                if hh == 1:
                    q0 = own * 256

                    def tr(on=on, ontk=ontk, q0=q0, hp=hp, own=own):
                        for qs in range(2):
                            self.pe(lambda e, on=on, qs=qs: e.transpose(ps7b[:, qs * 128:(qs + 1) * 128],
                                                                         on[:, qs, :], self.identb[:]),
                                    [ontk, "identb"], [("ps", 7)])
                        self.dve(lambda e, q0=q0, hp=hp: e.tensor_copy(out=OT[:, hp, q0:q0 + 256],
                                                                       in_=ps7b[:, 0:256]),
                                 [("ps", 7)], [("OT", hp, own // 2)])
                    pending.append(tr)

            prev = None
            pending = []

            def border(own):
                return ([own] + ([own - 1] if own >= 1 else []) + list(range(0, own - 1)))
            emit_mask_rows()
            LAG = 2
            for (own, hh) in units:
                par = cnt["unit"] % 2
                cnt["unit"] += 1
                pbi = par
                order = border(own)
                porder = border(prev[0]) if prev is not None else []
                npv = 0
                for i, n in enumerate(order):
                    emit_block_scores(own, hh, n, pbi)
                    if i >= LAG and npv < len(porder):
                        emit_pv_block(prev, porder[npv])
                        npv += 1
                    if own >= 5 and i in (3, 5) and next_chunks:
                        next_chunks.pop(0)()
                if prev is not None:
                    for n in porder[npv:]:
                        emit_pv_block(prev, n)
                    if pending:
                        pending.pop(0)()
                    emit_pv_finish(prev)
                prev = (own, hh, pbi, par)
            if prev is not None:
                for n in border(prev[0]):
                    emit_pv_block(prev, n)
                emit_pv_finish(prev)
            while next_chunks:
                next_chunks.pop(0)()
            while pending:
                pending.pop(0)()
        def load_o(oc):
            for hf in range(2):
                self.wload(wo[:, :, oc * 128 + hf * 64:oc * 128 + (hf + 1) * 64], qstg[:], "qstg",
                           qwb[oc % 2][:, :, hf * 64:(hf + 1) * 64], ("qwb", oc % 2))

        load_o(0)
        for oc in range(8):
            if oc + 1 < 8:
                load_o(oc + 1)
            for j in range(4):
                pb = 6 + j % 2
                for c in range(NCH):
                    self.mm(self.ps[pb][:], qwb[oc % 2][:, c, :], OT[:, c, tl(j)], c == 0, c == NCH - 1,
                            [("qwb", oc % 2), ("OT", c, j)], [("ps", pb)])
                if feed:
                    fflush(1)
                self.dve(lambda e, pb=pb, oc=oc, j=j: e.tensor_tensor(
                    out=self.XT[:, oc, tl(j)], in0=self.ps[pb][:], in1=self.XT[:, oc, tl(j)], op=ALU.add),
                    [("ps", pb), ("XT", oc, j)], [("XT", oc, j)])
                if feed:
                    fsq(oc, j)
        if feed:
            fflush(0)
        self.barrier()


_CACHE = {}


def _build(layers, final=True):
    key = (tuple(layers), final, DEBUG, ATT_STAGE, ATT_HP, ATT_OWN, HACK, FEED_ATT, FEED_FFN_ATT)
    if key not in _CACHE:
        nc = bass.Bass("TRN2", target_bir_lowering=False)
        b = Builder(nc, layers, final)
        b.build()
        _CACHE[key] = (nc, set(k for k in b.dr.keys()), list(b.dbgs.keys()))
    return _CACHE[key]


def _rel_bucket(n):
    n = np.asarray(n)
    nf = np.maximum(n, 16).astype(np.float32)
    large = 16 + (np.log(nf / np.float32(16)) / np.float32(np.log(128 / 16)) * np.float32(16)).astype(np.int32)
    large = np.minimum(large, 31)
    return np.where(n < 16, n, large)


def _host_inputs(norm_mixer, norm_ffn, pool_scale, kv_norm, rel_bias):
    f32 = np.float32
    cols = [np.asarray(norm_mixer, f32).reshape(4, 8, 128), np.asarray(norm_ffn, f32).reshape(4, 8, 128),
            np.asarray(pool_scale, f32).reshape(2, 8, 128), np.asarray(kv_norm, f32).reshape(1, 8, 128)]
    vecs = np.concatenate([c.reshape(-1, 128) for c in cols], axis=0).T.copy()
    assert vecs.shape == (128, NV)
    fw = np.zeros((128, 64), f32)
    for g in range(4):
        w = 2 << g
        fw[:, g * 16:(g + 1) * 16] = 1.0 / np.minimum(np.arange(16) + 1, w)
    p = np.arange(128)[:, None]
    c = np.arange(256)[None, :]
    dist = c - p
    bucket = _rel_bucket(np.maximum(dist, 0))
    rb = np.asarray(rel_bias, f32)
    gm = rb[bucket]
    gm = np.where((dist >= 0)[:, :, None], gm, f32(-30000.0))
    gmat = np.ascontiguousarray(gm.transpose(0, 2, 1)).astype(f32)
    return vecs, fw, gmat


def kernel(x, norm_mixer, norm_ffn, pool_w, pool_scale, kv_norm, w_kv, w_q, w_o,
           rel_bias, w_gate_up, w_down, final_norm, _layers=None):
    layers = list(range(DEPTH)) if _layers is None else _layers
    nc, used, dbgs = _build(layers)
    f32 = np.float32
    x = np.ascontiguousarray(x, dtype=f32)
    vecs, fw, gmat = _host_inputs(norm_mixer, norm_ffn, pool_scale, kv_norm, rel_bias)
    import ml_dtypes
    kind = np.zeros((128, T), f32)
    for n in range(8):
        kind[n, n * 256:(n + 1) * 256] = 1.0
        kind[64 + n, n * 256:(n + 1) * 256] = 1.0
    shared = {
        "vecs": vecs, "ident_in": np.eye(128, dtype=f32), "fw": fw, "kind": kind.astype(ml_dtypes.bfloat16),
        "final_norm": np.ascontiguousarray(final_norm, dtype=f32).reshape(1, D),
        "pool_w": np.ascontiguousarray(pool_w, dtype=f32), "w_kv": np.ascontiguousarray(w_kv, dtype=f32),
        "w_q": np.ascontiguousarray(w_q, dtype=f32), "w_o": np.ascontiguousarray(w_o, dtype=f32),
        "rel_bias": np.ascontiguousarray(rel_bias, dtype=f32), "gmat": gmat,
        "w_gate_up": np.ascontiguousarray(w_gate_up, dtype=f32),
        "w_down": np.ascontiguousarray(w_down, dtype=f32),
    }
    in_maps = []
    shared = {k: v for k, v in shared.items() if k in used}
    for b in range(NB):
        m = dict(shared)
        m["x"] = x[b]
        in_maps.append(m)
    res = run_bass_kernel_spmd(nc, in_maps, core_ids=list(range(NB)))
    for k in dbgs:
        LAST[k] = [np.asarray(res.results[b][k]) for b in range(NB)]
    out = np.stack([np.asarray(res.results[b]["out"]) for b in range(NB)], axis=0)
    return out.astype(f32)
```
